# Optimizing a Trainium2 kernel written in Bass

```python
import math
import jax, jax.numpy as jnp
from jax import lax
import numpy as np

D_MODEL = 2048
BATCH = 4
SEQ = 4096
DEPTH = 2

D_MIX = D_MODEL
A_HEADS = 8
A_QK_DIM = 64
A_V_DIM = 2 * A_QK_DIM
A_WIDTH = A_HEADS * A_V_DIM
B_GROUPS = 4
B_CHUNK = 128
B_WIDTH = D_MIX // 4
B_GROUP_DIM = B_WIDTH // B_GROUPS
C_GROUPS = 4
C_WIDTH = D_MIX - A_WIDTH - B_WIDTH
CONV_K = 3

ROPE_THETA = 10000.0
Q_BLOCK = 128
NORM_EPS = 1e-5
LN_EPS = 1e-5

PROJ_SIZES = (A_HEADS * 2 * A_QK_DIM, A_HEADS * 2 * A_QK_DIM, A_WIDTH, A_WIDTH,
              B_WIDTH, B_WIDTH, B_WIDTH,
              C_WIDTH, C_WIDTH, C_WIDTH, C_WIDTH)
PROJ_COLS = sum(PROJ_SIZES)

kernel_name = "hybrid_diffattn_sgu_shortconv"


def _split_points():
    pts, acc = [], 0
    for s in PROJ_SIZES[:-1]:
        acc += s
        pts.append(acc)
    return pts


def rms_norm(x, w):
    xf = x.astype(jnp.float32)
    y = xf * lax.rsqrt(jnp.mean(xf * xf, axis=-1, keepdims=True) + NORM_EPS)
    return (y * w.astype(jnp.float32)).astype(x.dtype)


def layer_norm(x, g, b):
    xf = x.astype(jnp.float32)
    mu = jnp.mean(xf, axis=-1, keepdims=True)
    xc = xf - mu
    var = jnp.mean(xc * xc, axis=-1, keepdims=True)
    y = xc * lax.rsqrt(var + LN_EPS) * g.astype(jnp.float32) + b.astype(jnp.float32)
    return y.astype(x.dtype)


def rotary_tables(seq, dim, dtype):
    pos = jnp.arange(seq, dtype=jnp.float32)
    inv_freq = ROPE_THETA ** (-jnp.arange(0, dim, 2, dtype=jnp.float32) / dim)
    ang = pos[:, None] * inv_freq[None, :]
    return jnp.cos(ang).astype(dtype), jnp.sin(ang).astype(dtype)


def apply_rotary(x, cos, sin):
    x1, x2 = jnp.split(x, 2, axis=-1)
    c = cos[None, :, None, None, :]
    s = sin[None, :, None, None, :]
    return jnp.concatenate([x1 * c - x2 * s, x2 * c + x1 * s], axis=-1)


def diff_attention(q, k, v, lam, cos, sin):
    b, s, h, _, d = q.shape
    q = apply_rotary(q, cos, sin)
    k = apply_rotary(k, cos, sin)
    scale = d ** -0.5
    nb = s // Q_BLOCK
    q_blocks = q.reshape(b, nb, Q_BLOCK, h, 2, d).swapaxes(0, 1)
    k32 = k.astype(jnp.float32)
    key_pos = jnp.arange(s)
    neg = jnp.finfo(jnp.float32).min

    def one_block(args):
        q_blk, start = args
        sc = jnp.einsum('bqhcd,bkhcd->bhcqk', q_blk.astype(jnp.float32), k32) * scale
        q_pos = start + jnp.arange(Q_BLOCK)
        causal = key_pos[None, :] <= q_pos[:, None]
        sc = jnp.where(causal, sc, neg)
        p = jax.nn.softmax(sc, axis=-1)
        attn = p[:, :, 0] - lam * p[:, :, 1]
        return jnp.einsum('bhqk,bkhe->bqhe', attn.astype(v.dtype), v)

    out = lax.map(one_block, (q_blocks, jnp.arange(nb) * Q_BLOCK))
    return out.swapaxes(0, 1).reshape(b, s, h, v.shape[-1])


def spatial_gating(u, v, ln_g, ln_b, w_s, b_s):
    b, s, _ = v.shape
    v = layer_norm(v, ln_g, ln_b)
    nc = s // B_CHUNK
    vc = v.reshape(b, nc, B_CHUNK, B_GROUPS, B_GROUP_DIM)
    w = jnp.tril(w_s)
    mixed = jnp.einsum('gts,bcsgd->bctgd', w, vc) + b_s.T[None, None, :, :, None]
    return u * mixed.reshape(b, s, B_WIDTH)


def short_conv(xc, bgate, cgate, conv_w):
    z = cgate * xc
    s = z.shape[1]
    zp = jnp.pad(z, ((0, 0), (CONV_K - 1, 0), (0, 0)))
    y = conv_w[0] * zp[:, 0:s]
    for i in range(1, CONV_K):
        y = y + conv_w[i] * zp[:, i:i + s]
    return bgate * y


def hybrid_layer(x, layer_idx, norm_w, w_in, lam_q1, lam_k1, lam_q2, lam_k2, subln_w,
                 sgu_ln_g, sgu_ln_b, w_s, b_s, conv_w, w_out, cos, sin):
    b, s, _ = x.shape
    h = rms_norm(x, norm_w)
    proj = h @ w_in
    (q, k, v, gate_a, u, v_s, gate_b, xc, bgate, cgate, gate_c) = jnp.split(
        proj, _split_points(), axis=-1)

    lam_init = 0.8 - 0.6 * math.exp(-0.3 * layer_idx)
    lam = (jnp.exp(jnp.sum(lam_q1.astype(jnp.float32) * lam_k1.astype(jnp.float32)))
           - jnp.exp(jnp.sum(lam_q2.astype(jnp.float32) * lam_k2.astype(jnp.float32)))
           + lam_init)
    q = q.reshape(b, s, A_HEADS, 2, A_QK_DIM)
    k = k.reshape(b, s, A_HEADS, 2, A_QK_DIM)
    v = v.reshape(b, s, A_HEADS, A_V_DIM)
    ya = diff_attention(q, k, v, lam, cos, sin)
    ya = rms_norm(ya, subln_w) * (1.0 - lam_init)
    ya = ya.reshape(b, s, A_WIDTH) * jax.nn.silu(gate_a)

    yb = spatial_gating(u, v_s, sgu_ln_g, sgu_ln_b, w_s, b_s) * jax.nn.silu(gate_b)

    yc = short_conv(xc, bgate, cgate, conv_w) * jax.nn.silu(gate_c)

    y = jnp.concatenate([ya, yb, yc], axis=-1) @ w_out
    return x + y


def setup_inputs(seed: int = 0) -> dict:
    key = jax.random.key(seed)
    ks = jax.random.split(key, 16)
    f32 = jnp.float32
    nrm = lambda k, shape: jax.random.normal(k, shape, dtype=f32)
    return {
        "x": nrm(ks[0], (BATCH, SEQ, D_MODEL)),
        "norm_w": 1.0 + 0.02 * nrm(ks[1], (DEPTH, D_MODEL)),
        "w_in": nrm(ks[2], (DEPTH, D_MODEL, PROJ_COLS)) * D_MODEL ** -0.5,
        "lam_q1": 0.1 * nrm(ks[3], (DEPTH, A_QK_DIM)),
        "lam_k1": 0.1 * nrm(ks[4], (DEPTH, A_QK_DIM)),
        "lam_q2": 0.1 * nrm(ks[5], (DEPTH, A_QK_DIM)),
        "lam_k2": 0.1 * nrm(ks[6], (DEPTH, A_QK_DIM)),
        "subln_w": 1.0 + 0.02 * nrm(ks[7], (DEPTH, A_V_DIM)),
        "sgu_ln_g": 1.0 + 0.02 * nrm(ks[8], (DEPTH, B_WIDTH)),
        "sgu_ln_b": 0.02 * nrm(ks[9], (DEPTH, B_WIDTH)),
        "w_s": nrm(ks[10], (DEPTH, B_GROUPS, B_CHUNK, B_CHUNK)) * B_CHUNK ** -0.5,
        "b_s": 1.0 + 0.1 * nrm(ks[11], (DEPTH, B_GROUPS, B_CHUNK)),
        "conv_w": nrm(ks[12], (DEPTH, CONV_K, C_WIDTH)) * CONV_K ** -0.5,
        "w_out": nrm(ks[13], (DEPTH, D_MIX, D_MODEL)) * D_MIX ** -0.5,
        "final_norm_w": 1.0 + 0.02 * nrm(ks[14], (D_MODEL,)),
    }


def reference(x, norm_w, w_in, lam_q1, lam_k1, lam_q2, lam_k2, subln_w, sgu_ln_g, sgu_ln_b,
              w_s, b_s, conv_w, w_out, final_norm_w):
    cos, sin = rotary_tables(x.shape[1], A_QK_DIM, x.dtype)
    for l in range(DEPTH):
        x = hybrid_layer(x, l, norm_w[l], w_in[l], lam_q1[l], lam_k1[l], lam_q2[l], lam_k2[l],
                         subln_w[l], sgu_ln_g[l], sgu_ln_b[l], w_s[l], b_s[l], conv_w[l],
                         w_out[l], cos, sin)
    return rms_norm(x, final_norm_w)
```

```python
import math
import os
from contextlib import ExitStack

import numpy as np
import concourse.bass as bass
import concourse.mybir as mybir
from concourse.bass_utils import run_bass_kernel_spmd

F32 = mybir.dt.float32
BF16 = mybir.dt.bfloat16
AF = mybir.ActivationFunctionType
ALU = mybir.AluOpType
AX = mybir.AxisListType

DEPTH = 2
D = 2048
T = 2048
NTB = 16
NTT = 4
KC = 16
PROJ = 7680
NH = 8
NORM_EPS = 1e-5
LN_EPS = 1e-5
GROUPS = [[0, 1], [2, 3], [4, 5], [6, 7]]
SEM_LIMIT = 6000
PSUM_EXCL = True


class _Stop(Exception):
    pass


class Sched:
    def __init__(self):
        self.ops = []

    def add(self, eng, fn, reads=(), writes=(), dma=None, inc=16):
        self.ops.append(dict(eng=eng, fn=fn, reads=tuple(reads), writes=tuple(writes),
                             dma=dma, inc=inc, deps=(), sig=False, sigval=None, waits=()))

    def fence(self, key):
        self.add('pool', self._fence_fn, reads=(), writes=(key, "_fence_scratch"))
        self.ops[-1]['barrier'] = True

    def resolve(self, new_sem):
        ops = self.ops
        W = {}
        R = {}
        last_eng = {}
        dma_since = []
        last_bar = None
        for i, op in enumerate(ops):
            deps = set()
            if op.get('barrier'):
                deps.update(last_eng.values())
                deps.update(dma_since)
            if last_bar is not None:
                deps.add(last_bar)
            for k in op['reads']:
                w = W.get(k)
                if w:
                    deps.update(w['eng'].values())
                    deps.update(w['dma'])
                if PSUM_EXCL and k.startswith("ps") and k[2:].isdigit():
                    r = R.get(k)
                    if r:
                        deps.update(j for e2, j in r['eng'].items() if e2 != op['eng'])
            for k in op['writes']:
                w = W.get(k)
                if w:
                    deps.update(w['eng'].values())
                    deps.update(w['dma'])
                r = R.get(k)
                if r:
                    deps.update(r['eng'].values())
                    deps.update(r['dma'])
            deps.discard(i)
            keep = []
            for j in deps:
                oj = ops[j]
                if oj['eng'] == 'pe' and op['eng'] == 'pe' and oj['dma'] is None and op['dma'] is None:
                    continue
                keep.append(j)
                oj['sig'] = True
            op['deps'] = keep
            if op.get('barrier'):
                last_bar = i
                dma_since = []
                op['sig'] = True
            if op['dma'] is not None:
                dma_since.append(i)
            else:
                last_eng[op['eng']] = i
            for k in op['reads']:
                r = R.setdefault(k, {'eng': {}, 'dma': []})
                if op['dma'] is not None:
                    r['dma'].append(i)
                else:
                    r['eng'][op['eng']] = i
            for k in op['writes']:
                if op['dma'] is not None:
                    w = W.setdefault(k, {'eng': {}, 'dma': []})
                    w['eng'] = {}
                    w['dma'].append(i)
                else:
                    W[k] = {'eng': {op['eng']: i}, 'dma': []}
                R[k] = {'eng': {}, 'dma': []}
        cnt = {}
        cur = {}
        dcnt = {}
        dsem = {}
        for i, op in enumerate(ops):
            waits = {}
            for j in op['deps']:
                oj = ops[j]
                if oj['dma'] is not None:
                    s, v = dsem[oj['dma']], dcnt[oj['dma']]
                else:
                    s, v = oj['sigval']
                if waits.get(s, (None, 0))[1] < v:
                    waits[s] = (s, v)
            op['waits'] = list(waits.values())
            if op['dma'] is not None:
                k = op['dma']
                if k not in dsem:
                    dsem[k] = new_sem("d_" + k)
                    dcnt[k] = 0
                dcnt[k] += op['inc']
                op['sigval'] = (dsem[k], dcnt[k])
            elif op['sig']:
                e = op['eng']
                if e not in cur or cnt[e] >= SEM_LIMIT:
                    cur[e] = new_sem("c_%s_%d" % (e, i))
                    cnt[e] = 0
                cnt[e] += 1
                op['sigval'] = (cur[e], cnt[e])

    def emit_engine(self, eng, eobj):
        known = {}
        n = 0
        for op in self.ops:
            if op['eng'] != eng:
                continue
            for (s, v) in op['waits']:
                kid = id(s)
                if known.get(kid, 0) >= v:
                    continue
                known[kid] = v
                eobj.wait_ge(s, v)
            ins = op['fn'](eobj)
            n += 1
            if op['dma'] is not None:
                ins.then_inc(op['sigval'][0], op['inc'])
            elif op['sig']:
                ins.then_inc(op['sigval'][0], 1)
        return n


def build_program(layer_ids, final_norm, dbg=None):
    L = len(layer_ids)
    nc = bass.Bass("TRN2", target_bir_lowering=False)
    S = Sched()
    es = ExitStack()

    def din(name, shape, dt=F32):
        return nc.dram_tensor(name, list(shape), dt, kind="ExternalInput")

    x_in = din("x", [T, D])
    w_in = din("w_in", [L, D, PROJ])
    w_out = din("w_out", [L, D, D])
    nwb_in = din("nwb", [L, 128, D])
    fnwb_in = din("fnwb", [128, D])
    cos_in = din("cosT", [128, T])
    sin_in = din("sinT", [128, T])
    lamv_in = din("lamv", [L, 128, 256])
    slnw_in = din("slnw", [L, 128, 128])
    lng_in = din("lng", [L, 128, 512])
    lnb_in = din("lnb", [L, 128, 512])
    wsT_in = din("wsT", [L, 128, 512])
    bsb_in = din("bsb", [L, 128, 512])
    cw_in = din("cw", [L, 128, 12])
    flag_in = din("flag", [128, 1])
    ident_in = din("ident", [128, 128])
    rotR_in = din("rotR", [128, 128])
    mask_in = din("mask", [128, 128])
    out_t = nc.dram_tensor("out", [T, D], F32, kind="ExternalOutput")

    xres = [nc.dram_tensor("xres%d" % i, [T, D], F32) for i in range(max(L - 1, 0))]
    kvin = [[nc.dram_tensor("kvin%d_%d" % (l, h), [256, 2048], BF16) for h in range(NH)] for l in range(L)]
    kvout = [[nc.dram_tensor("kvout%d_%d" % (l, h), [512, 2048], BF16) for h in range(NH)] for l in range(L)]
    q_s = [nc.dram_tensor("qs%d" % l, [1024, 2048], BF16) for l in range(L)]
    ga_s = [nc.dram_tensor("gas%d" % l, [1024, 2048], BF16) for l in range(L)]
    zt_i = [nc.dram_tensor("zti%d" % l, [128, 8], F32) for l in range(L)]
    zt_o = [nc.dram_tensor("zto%d" % l, [256, 8], F32) for l in range(L)]

    def sb(name, shape, dt=F32):
        return es.enter_context(nc.sbuf_tensor(name + "_sb", list(shape), dt))

    R1 = sb("R1", [128, 16, 2048], BF16)
    R2lo = sb("R2lo", [128, 4, 2048], F32)
    R2hi = sb("R2hi", [128, 8, 2048], BF16)
    R34 = sb("R34", [128, 16384], BF16)
    R5 = sb("R5", [128, 4096], F32)
    qbf = [sb("qbf%d" % i, [128, 512], BF16) for i in range(2)]
    vn = sb("vn", [128, 512], BF16)
    ident = sb("ident", [128, 128], F32)
    rotR = sb("rotRb", [128, 128], BF16)
    maskb = sb("maskb", [128, 128], BF16)
    maskf = sb("maskf", [128, 128], F32)
    flag = sb("flagt", [128, 1], F32)
    lamv = sb("lamvt", [128, 256], F32)
    lamt = sb("lamt", [128, 8], F32)
    slnw = sb("slnwt", [128, 128], F32)
    wsTb = sb("wsTb", [128, 4, 128], BF16)
    bsb = sb("bsbt", [128, 4, 128], F32)
    cw = sb("cwt", [128, 4, 3], F32)
    stA = sb("stA", [128, 48], F32)
    stB = sb("stB", [128, 8], F32)
    stC = sb("stC", [128, 32], F32)
    gg01 = sb("gg01", [128, 4, 2], F32)
    ztail = sb("ztail", [128, 4, 2], F32)
    hz = sb("hz", [128, 4, 2], F32)
    corr = sb("corr", [128, 4, 2], F32)
    fsc = sb("fsc", [128, 1], F32)
    junk2 = sb("junk2", [128, 512], BF16)

    ps = [es.enter_context(nc.psum_tensor("ps%d" % i, [128, 512], F32)) for i in range(8)]

    S._fence_fn = lambda e: e.memset(fsc[:, 0:1], 0.0)

    ya = R2lo[:, :, :].bitcast(BF16).rearrange("p g (two n) -> p (g two) n", two=2)
    stg = [ya[:, 0, :], ya[:, 1, :]]
    vst = [ya[:, 2, :].rearrange("p (b n) -> p b n", b=4), ya[:, 3, :].rearrange("p (b n) -> p b n", b=4)]
    wbuf = [R34[:, s * 8192:(s + 1) * 8192].rearrange("p (k n) -> p k n", k=16) for s in range(2)]
    nwb = R34[:, 8192:12288].bitcast(F32)
    junk = R34[:, 12288:14336]
    kTl = R34[:, 0:2048]
    kTr = R34[:, 2048:4096]
    qT = R34[:, 4096:6144]
    gaT = R34[:, 6144:8192]
    Vl = R34[:, 8192:8192 + 2064].rearrange("p (b n) -> p b n", b=16)
    Vrr = R34[:, 10256:10256 + 2064].rearrange("p (b n) -> p b n", b=16)
    Vr = R34[:, 12320:12320 + 2064].rearrange("p (b n) -> p b n", b=16)
    Pb = [R34[:, 14384 + i * 512:14384 + (i + 1) * 512] for i in range(3)]
    xb = [R5[:, 0:2048], R5[:, 2048:4096]]
    T5 = [R5[:, i * 512:(i + 1) * 512] for i in range(8)]
    TR = [R2lo[:, 2, i * 512:(i + 1) * 512] for i in range(4)]
    lng, lnb = T5[6], T5[7]
    wsTf = R34[:, 0:1024].bitcast(F32).rearrange("p (g n) -> p g n", g=4)
    cosT, sinT = xb[0], xb[1]

    def dma(eng, out, in_, key, reads, writes):
        S.add(eng, lambda e: e.dma_start(out=out, in_=in_), reads, writes, dma=key)

    def mm(out, lhsT, rhs, start, stop, reads, writes):
        S.add('pe', lambda e: e.matmul(out, lhsT, rhs, start=start, stop=stop), reads, writes)

    def tr(out, in_, reads, writes):
        S.add('pe', lambda e: e.transpose(out, in_, ident[:, :]), reads, writes)

    def act(out, in_, func, reads, writes, bias=None, scale=None, accum_out=None):
        kw = {}
        if bias is not None:
            kw['bias'] = bias
        if scale is not None:
            kw['scale'] = scale
        if accum_out is not None:
            kw['accum_out'] = accum_out
        S.add('act', lambda e: e.activation(out, in_, func, **kw), reads, writes)

    def tt(eng, out, in0, in1, op, reads, writes):
        S.add(eng, lambda e: e.tensor_tensor(out, in0, in1, op), reads, writes)

    def ts(eng, out, in0, s1, s2, op0, op1, reads, writes):
        if op1 is None:
            S.add(eng, lambda e: e.tensor_scalar(out, in0, s1, None, op0), reads, writes)
        else:
            S.add(eng, lambda e: e.tensor_scalar(out, in0, s1, s2, op0, op1), reads, writes)

    def stt(out, in0, scalar, in1, op0, op1, reads, writes):
        S.add('dve', lambda e: e.scalar_tensor_tensor(out, in0, scalar, in1, op0, op1), reads, writes)

    def recip(out, in_, reads, writes):
        S.add('dve', lambda e: e.reciprocal(out, in_), reads, writes)

    def cp(eng, out, in_, reads, writes):
        S.add(eng, lambda e: e.tensor_copy(out, in_), reads, writes)

    def memset(eng, ap, val, reads, writes):
        S.add(eng, lambda e: e.memset(ap, val), reads, writes)

    bank_ctr = [0]

    def nbank():
        b = bank_ctr[0] % 8
        bank_ctr[0] += 1
        return b

    dbg_t = nc.dram_tensor("dbg", [128, 8 * 2048], F32, kind="ExternalOutput") if dbg else None
    dbg_keys = []

    def dump(slot, src_ap, dram=False):
        s2 = slot % 2
        if dram:
            dma('sp', R34[:, 0:2048], src_ap, "dstage", [], ["dstage"])
            src_ap = R34[:, 0:2048]
        cp('dve', xb[s2], src_ap, ["dstage"], ["xbd%d" % s2])
        dma('sp', dbg_t[:, slot * 2048:(slot + 1) * 2048], xb[s2], "dbg%d" % s2, ["xbd%d" % s2], ["dbgout%d" % slot])
        dbg_keys.append("dbgout%d" % slot)

    dma('sp', ident[:, :], ident_in[:, :], "cst", [], ["ident"])
    dma('sp', maskf[:, :], mask_in[:, :], "cst", [], ["maskf"])
    dma('sp', flag[:, :], flag_in[:, :], "cst", [], ["flag"])
    dma('pool', rotR[:, :], rotR_in[:, :], "cstp", [], ["rotR"])
    dma('pool', maskb[:, :], mask_in[:, :], "cstp", [], ["maskb"])

    hT_keys = ["hT%d_%d" % (tb, q4) for tb in range(NTB) for q4 in range(4)]

    try:
        for li, layer in enumerate(layer_ids):
            lam_init = 0.8 - 0.6 * math.exp(-0.3 * layer)
            last = (li == L - 1)
            x_src = x_in if li == 0 else xres[li - 1]
            x_dst = out_t if last else xres[li]
            do_final = last and final_norm
            LK = "L%d" % li

            def xkey(t, tb):
                return "dram_%s_%d" % (t.name if hasattr(t, "name") else id(t), tb)

            S.fence("R34gen")
            dma('sp', nwb, nwb_in[li, :, :], "sm", ["R34gen"], ["nwb"])
            dma('sp', lamv[:, :], lamv_in[li, :, :], "sm", [], ["lamv"])
            dma('sp', slnw[:, :], slnw_in[li, :, :], "sm", [], ["slnw"])
            dma('sp', wsTf, wsT_in[li, :, :].rearrange("p (g n) -> p g n", g=4), "sm", [], ["wsTf"])
            dma('sp', bsb[:, :, :], bsb_in[li, :, :].rearrange("p (g n) -> p g n", g=4), "sm", [], ["bsb"])
            dma('sp', cw[:, :, :], cw_in[li, :, :].rearrange("p (g n) -> p g n", g=4), "sm", [], ["cw"])
            nlam = lamt[:, 5:6]
            _sk = os.environ.get('SKIPSM', '0')
            if True:
                if _sk not in ('1', '2'):
                    tt('dve', lamv[:, 0:64], lamv[:, 0:64], lamv[:, 64:128], ALU.mult, ["lamv"], ["lamp1"])
                    tt('dve', lamv[:, 128:192], lamv[:, 128:192], lamv[:, 192:256], ALU.mult, ["lamv"], ["lamp2"])
                    if _sk != '4':
                        act(junk2[:, 0:64], lamv[:, 0:64], AF.Copy, ["lamp1"], ["junk2", "lams1"], accum_out=lamt[:, 0:1])
                        act(junk2[:, 0:64], lamv[:, 128:192], AF.Copy, ["lamp2"], ["junk2", "lams2"], accum_out=lamt[:, 1:2])
                        act(lamt[:, 2:3], lamt[:, 0:1], AF.Exp, ["lams1"], ["lame1"])
                        act(lamt[:, 3:4], lamt[:, 1:2], AF.Exp, ["lams2"], ["lame2"])
                    if _sk not in ('4', '5'):
                        tt('dve', lamt[:, 4:5], lamt[:, 3:4], lamt[:, 2:3], ALU.subtract, ["lame1", "lame2"], ["lamd"])
                        ts('dve', lamt[:, 5:6], lamt[:, 4:5], -lam_init, None, ALU.add, None, ["lamd"], ["nlam"])
                nlam = lamt[:, 5:6]
                if _sk not in ('1', '3'):
                    for g in range(4):
                        tt('dve', wsTb[:, g, :], wsTf[:, g, :], maskf[:, :], ALU.mult, ["wsTf", "maskf"], ["wsTb%d" % g])
                    ts('dve', slnw[:, :], slnw[:, :], 1.0 - lam_init, None, ALU.mult, None, ["slnw"], ["slnw"])


            S.fence("R1gen")
            S.fence("R5gen")
            for tb in range(0 if os.environ.get('SKIPA') else NTB):
                s = tb % 2
                xk = ["xb%d_%d" % (s, c) for c in range(4)]
                dma('sp', xb[s], x_src[tb * 128:(tb + 1) * 128, :], "xb%d" % s,
                    ["xrow_%d_%d" % (li, tb), "R5gen"], xk)
                act(junk, xb[s], AF.Square, xk + ["R34gen"], ["junk", "ssA%d" % tb], accum_out=stA[:, tb:tb + 1])
                act(stA[:, 16 + tb:17 + tb], stA[:, tb:tb + 1], AF.Sqrt, ["ssA%d" % tb], ["sdA%d" % tb],
                    scale=1.0 / D, bias=NORM_EPS)
                recip(stA[:, 32 + tb:33 + tb], stA[:, 16 + tb:17 + tb], ["sdA%d" % tb], ["rsA%d" % tb])
                stt(xb[s], xb[s], stA[:, 32 + tb:33 + tb], nwb, ALU.mult, ALU.mult,
                    xk + ["rsA%d" % tb, "nwb"], xk)
                for q4 in range(4):
                    b = nbank()
                    for j in range(4):
                        c = q4 * 4 + j
                        tr(ps[b][:, j * 128:(j + 1) * 128], xb[s][:, c * 128:(c + 1) * 128],
                           xk + ["ident"], ["ps%d" % b])
                    eng = 'dve' if q4 % 2 == 0 else 'act'
                    outv = R1[:, q4 * 4:(q4 + 1) * 4, tb * 128:(tb + 1) * 128]
                    inv = ps[b][:, :].rearrange("p (j n) -> p j n", j=4)
                    if eng == 'dve':
                        cp('dve', outv, inv, ["ps%d" % b, "R1gen"], ["hT%d_%d" % (tb, q4)])
                    else:
                        act(outv, inv, AF.Copy, ["ps%d" % b, "R1gen"], ["hT%d_%d" % (tb, q4)])

            if dbg == 'A':
                S.fence("dbgf")
                dump(0, R1[:, 0, :])
                dump(1, R1[:, 15, :])
                raise _Stop()
            S.fence("R34gen")
            S.fence("R5gen")
            S.fence("R2gen")
            S.fence("TMPgen")
            dma('sp', cosT, cos_in[:, :], "cs", ["R5gen"], ["cosT"])
            dma('sp', sinT, sin_in[:, :], "cs", ["R5gen"], ["sinT"])
            wslot = [0]

            def load_w(blocks):
                s = wslot[0] % 2
                wslot[0] += 1
                off = 0
                for (c0, n) in blocks:
                    for kh in range(2):
                        dma('pool', wbuf[s][:, kh * 8:(kh + 1) * 8, off:off + n],
                            w_in[li, kh * 1024:(kh + 1) * 1024, c0:c0 + n].rearrange("(k p) n -> p k n", p=128),
                            "w%d" % s, ["R34gen"], ["w%d" % s])
                    off += n
                return s

            def fm_block(s, cb, t):
                b = nbank()
                rk = ["w%d" % s] + ["hT%d_%d" % (tb, q) for tb in range(4 * t, 4 * t + 4) for q in range(4)]
                for k in range(KC):
                    mm(ps[b][:, :], wbuf[s][:, k, cb * 128:(cb + 1) * 128], R1[:, k, t * 512:(t + 1) * 512],
                       k == 0, k == KC - 1, rk, ["ps%d" % b])
                return b

            def tm_block(s, tb):
                b = nbank()
                rk = ["w%d" % s] + ["hT%d_%d" % (tb, q) for q in range(4)]
                for k in range(KC):
                    mm(ps[b][:, :], R1[:, k, tb * 128:(tb + 1) * 128], wbuf[s][:, k, 0:512],
                       k == 0, k == KC - 1, rk, ["ps%d" % b])
                return b

            stg_ctr = [0]
            dbg_it = [0]

            def rope_job(col0, dst_fn, dkey_fn):
                s = load_w([(col0, 512)])
                dk = os.environ.get("DBGK")
                dk_n, dk_lv = (int(dk.split(",")[0]), int(dk.split(",")[1])) if dk else (None, 9)
                for cb in range(4):
                    sg = stg_ctr[0] % 2
                    stg_ctr[0] += 1
                    for t in range(NTT):
                        if dk_n is not None and dbg_it[0] >= dk_n:
                            S.fence("dbgf")
                            dump(0, stg[0])
                            raise _Stop()
                        dbg_it[0] += 1
                        b = fm_block(s, cb, t)
                        qi = t % 2
                        if dk_lv == 0:
                            act(stg[sg][:, t * 512:(t + 1) * 512], ps[b][:, :], AF.Copy, ["ps%d" % b], ["stg%d_%d" % (sg, t)])
                            continue
                        act(qbf[qi][:, :], ps[b][:, :], AF.Copy, ["ps%d" % b], ["qbf%d" % qi])
                        b2 = nbank()
                        mm(ps[b2][:, :], rotR[:, :], qbf[qi][:, :], True, True, ["rotR", "qbf%d" % qi], ["ps%d" % b2])
                        t1 = TR[2 * qi]
                        t2 = TR[2 * qi + 1]
                        if dk_lv == 1:
                            act(stg[sg][:, t * 512:(t + 1) * 512], ps[b2][:, :], AF.Copy, ["ps%d" % b2], ["stg%d_%d" % (sg, t)])
                            continue
                        if dk_lv == 6:
                            cp('dve', t1, ps[b][:, :], ["ps%d" % b], ["T%d" % (2 * qi)])
                            act(stg[sg][:, t * 512:(t + 1) * 512], ps[b][:, :], AF.Copy, ["ps%d" % b], ["stg%d_%d" % (sg, t)])
                            continue
                        if dk_lv == 7:
                            act(stg[sg][:, t * 512:(t + 1) * 512], t1, AF.Copy, ["ps%d" % b], ["stg%d_%d" % (sg, t)])
                            continue
                        if dk_lv in (2, 4, 5):
                            if dk_lv == 2:
                                tt('dve', t1, ps[b][:, :], cosT[:, t * 512:(t + 1) * 512], ALU.mult,
                                   ["ps%d" % b, "cosT", "TMPgen"], ["T%d" % (2 * qi)])
                            elif dk_lv == 4:
                                tt('dve', t1, ps[b][:, :], ps[b][:, :], ALU.mult, ["ps%d" % b], ["T%d" % (2 * qi)])
                            else:
                                cp('dve', t1, ps[b][:, :], ["ps%d" % b], ["T%d" % (2 * qi)])
                            act(stg[sg][:, t * 512:(t + 1) * 512], t1, AF.Copy, ["T%d" % (2 * qi)], ["stg%d_%d" % (sg, t)])
                            continue
                        tt('dve', t1, ps[b][:, :], cosT[:, t * 512:(t + 1) * 512], ALU.mult,
                           ["ps%d" % b, "cosT", "TMPgen"], ["T%d" % (2 * qi)])
                        tt('dve', t2, ps[b2][:, :], sinT[:, t * 512:(t + 1) * 512], ALU.mult,
                           ["ps%d" % b2, "sinT", "TMPgen"], ["T%d" % (2 * qi + 1)])
                        tt('pool', stg[sg][:, t * 512:(t + 1) * 512], t1, t2, ALU.add,
                           ["T%d" % (2 * qi), "T%d" % (2 * qi + 1), "R2gen"], ["stg%d_%d" % (sg, t)])
                    dma('sp', dst_fn(cb), stg[sg], "stg%d" % sg, ["stg%d_%d" % (sg, t) for t in range(4)],
                        [dkey_fn(cb)])

            for hh in range(2):
                rope_job(1024 + hh * 512,
                         lambda cb, hh=hh: kvin[li][hh * 4 + cb][0:128, :],
                         lambda cb, hh=hh: "kvin%d_%d" % (li, hh * 4 + cb))
            if dbg == 'B1':
                S.fence("dbgf")
                dump(1, kvin[li][0][0:128, :], dram=True)
                raise _Stop()
            vst_ctr = [0]
            for hh in range(2):
                s = load_w([(2048 + hh * 512, 512)])
                for tq in range(4):
                    vs_ = vst_ctr[0] % 2
                    vst_ctr[0] += 1
                    for j in range(4):
                        tb = tq * 4 + j
                        b = tm_block(s, tb)
                        eng = 'dve' if j % 2 == 0 else 'act'
                        if eng == 'dve':
                            cp('dve', vst[vs_][:, j, :], ps[b][:, :], ["ps%d" % b, "R2gen"], ["vst%d_%d" % (vs_, j)])
                        else:
                            act(vst[vs_][:, j, :], ps[b][:, :], AF.Copy, ["ps%d" % b, "R2gen"], ["vst%d_%d" % (vs_, j)])
                    for cb in range(4):
                        h = hh * 4 + cb
                        dstv = kvin[li][h][128:256, :].rearrange("r (s e) -> (r s) e", e=128)
                        dstv = dstv[tq * 512:(tq + 1) * 512, :].rearrange("(b p) e -> p b e", p=128)
                        dma('sp', dstv, vst[vs_][:, :, cb * 128:(cb + 1) * 128], "vst%d" % vs_,
                            ["vst%d_%d" % (vs_, j) for j in range(4)], ["kvin%d_%d" % (li, h)])
            for h in range(NH):
                S.add('pool',
                      lambda e, h=h, li=li: e.collective_compute("AllGather", ALU.bypass, replica_groups=GROUPS,
                                                                 ins=[kvin[li][h].ap().opt()],
                                                                 outs=[kvout[li][h].ap().opt()]),
                      ["kvin%d_%d" % (li, h)], ["kvout%d_%d" % (li, h)], dma="cc", inc=1)
            if dbg == 'B2':
                S.fence("dbgf")
                dump(1, kvin[li][0][0:128, :], dram=True)
                dump(2, kvout[li][0][0:128, :], dram=True)
                dump(3, kvin[li][0][128:256, :], dram=True)
                raise _Stop()
            for hh in range(2):
                rope_job(hh * 512,
                         lambda cb, hh=hh: q_s[li][(hh * 4 + cb) * 128:(hh * 4 + cb + 1) * 128, :],
                         lambda cb, hh=hh: "qs%d_%d" % (li, hh * 4 + cb))
            for hh in range(2):
                s = load_w([(3072 + hh * 512, 512)])
                for cb in range(4):
                    h = hh * 4 + cb
                    sg = stg_ctr[0] % 2
                    stg_ctr[0] += 1
                    for t in range(NTT):
                        b = fm_block(s, cb, t)
                        act(stg[sg][:, t * 512:(t + 1) * 512], ps[b][:, :], AF.Silu, ["ps%d" % b, "R2gen"],
                            ["stg%d_%d" % (sg, t)])
                    dma('sp', ga_s[li][h * 128:(h + 1) * 128, :], stg[sg], "stg%d" % sg,
                        ["stg%d_%d" % (sg, t) for t in range(4)], ["gas%d_%d" % (li, h)])
            if dbg == 'B3':
                S.fence("dbgf")
                dump(0, q_s[li][0:128, :], dram=True)
                dump(1, kvin[li][0][0:128, :], dram=True)
                dump(2, kvout[li][0][0:128, :], dram=True)
                dump(3, kvin[li][0][128:256, :], dram=True)
                dump(4, ga_s[li][0:128, :], dram=True)
                raise _Stop()
            S.fence("R2gen")
            S.fence("TMPgen")
            mixb = R2lo
            dma('sp', lng, lng_in[li, :, :], "sm2", [], ["lng"])
            dma('sp', lnb, lnb_in[li, :, :], "sm2", [], ["lnb"])
            s = load_w([(4608, 512)])
            for tb in range(NTB):
                b = tm_block(s, tb)
                pk = ["ps%d" % b]
                act(junk2[:, :], ps[b][:, :], AF.Copy, pk, ["junk2", "lnsm"], accum_out=stB[:, 0:1])
                act(junk2[:, :], ps[b][:, :], AF.Square, pk, ["junk2", "lnsq"], accum_out=stB[:, 1:2])
                ts('dve', stB[:, 2:3], stB[:, 0:1], 1.0 / 512, None, ALU.mult, None, ["lnsm"], ["lnmean"])
                tt('dve', stB[:, 3:4], stB[:, 2:3], stB[:, 2:3], ALU.mult, ["lnmean"], ["lnmsq"])
                stt(stB[:, 4:5], stB[:, 1:2], 1.0 / 512, stB[:, 3:4], ALU.mult, ALU.subtract,
                    ["lnsq", "lnmsq"], ["lnvar"])
                act(stB[:, 5:6], stB[:, 4:5], AF.Sqrt, ["lnvar"], ["lnsd"], bias=LN_EPS)
                recip(stB[:, 6:7], stB[:, 5:6], ["lnsd"], ["lnrs"])
                ts('dve', T5[0], ps[b][:, :], stB[:, 2:3], stB[:, 6:7], ALU.subtract, ALU.mult,
                   pk + ["lnmean", "lnrs", "TMPgen"], ["T0"])
                tt('pool', T5[1], T5[0], lng, ALU.mult, ["T0", "lng", "TMPgen"], ["T1"])
                tt('pool', vn[:, :], T5[1], lnb, ALU.add, ["T1", "lnb"], ["vn"])
                b2 = nbank()
                for g in range(4):
                    mm(ps[b2][:, g * 128:(g + 1) * 128], vn[:, g * 128:(g + 1) * 128], wsTb[:, g, :], True, True,
                       ["vn", "wsTb%d" % g], ["ps%d" % b2])
                tt('dve', mixb[:, :, tb * 128:(tb + 1) * 128], ps[b2][:, :].rearrange("p (g n) -> p g n", g=4),
                   bsb[:, :, :], ALU.add, ["ps%d" % b2, "bsb", "R2gen"], ["mixb%d" % tb])
            for gp in range(2):
                s = load_w([(4096 + (2 * gp) * 128, 128), (5120 + (2 * gp) * 128, 128),
                            (4096 + (2 * gp + 1) * 128, 128), (5120 + (2 * gp + 1) * 128, 128)])
                for gi in range(2):
                    g = 2 * gp + gi
                    for t in range(NTT):
                        bu = fm_block(s, 2 * gi, t)
                        bg = fm_block(s, 2 * gi + 1, t)
                        act(T5[2], ps[bg][:, :], AF.Silu, ["ps%d" % bg, "TMPgen"], ["T2"])
                        tt('dve', T5[3], ps[bu][:, :], mixb[:, g, t * 512:(t + 1) * 512], ALU.mult,
                           ["ps%d" % bu, "TMPgen"] + ["mixb%d" % tb for tb in range(4 * t, 4 * t + 4)], ["T3"])
                        tt('pool', R2hi[:, g, t * 512:(t + 1) * 512], T5[2], T5[3], ALU.mult,
                           ["T2", "T3"], ["yb%d_%d" % (g, t)])
            if dbg == 'B4':
                S.fence("dbgf")
                dump(0, q_s[li][0:128, :], dram=True)
                dump(1, kvin[li][0][0:128, :], dram=True)
                dump(2, kvout[li][0][0:128, :], dram=True)
                dump(3, kvin[li][0][128:256, :], dram=True)
                dump(4, ga_s[li][0:128, :], dram=True)
                dump(5, R2hi[:, 0, :])
                raise _Stop()
            S.fence("TMPgen")
            zbuf = R5[:, 2048:3072]
            for c in range(4):
                s = load_w([(5632 + c * 128, 128), (6656 + c * 128, 128), (6144 + c * 128, 128), (7168 + c * 128, 128)])
                memset('pool', zbuf[:, 0:2], 0.0, ["TMPgen", "zcarry"], ["zcarry"])
                for t in range(NTT):
                    bx = fm_block(s, 0, t)
                    bc = fm_block(s, 1, t)
                    bb = fm_block(s, 2, t)
                    bgc = fm_block(s, 3, t)
                    act(T5[0], ps[bx][:, :], AF.Copy, ["ps%d" % bx, "TMPgen"], ["T0"])
                    tt('dve', zbuf[:, 2:514], ps[bc][:, :], T5[0], ALU.mult, ["ps%d" % bc, "T0", "TMPgen"], ["zmain"])
                    ts('dve', T5[1], zbuf[:, 2:514], cw[:, c, 2:3], None, ALU.mult, None,
                       ["zmain", "cw", "TMPgen"], ["T1"])
                    stt(T5[1], zbuf[:, 1:513], cw[:, c, 1:2], T5[1], ALU.mult, ALU.add,
                        ["zmain", "zcarry", "cw", "T1"], ["T1"])
                    stt(T5[1], zbuf[:, 0:512], cw[:, c, 0:1], T5[1], ALU.mult, ALU.add,
                        ["zmain", "zcarry", "cw", "T1"], ["T1"])
                    act(T5[2], ps[bgc][:, :], AF.Silu, ["ps%d" % bgc, "TMPgen"], ["T2"])
                    tt('dve', T5[3], ps[bb][:, :], T5[2], ALU.mult, ["ps%d" % bb, "T2", "TMPgen"], ["T3"])
                    tt('pool', R2hi[:, 4 + c, t * 512:(t + 1) * 512], T5[1], T5[3], ALU.mult,
                       ["T1", "T3"], ["yc%d_%d" % (c, t)])
                    if t == 0:
                        cp('pool', gg01[:, c, :], T5[3][:, 0:2], ["T3"], ["gg01_%d" % c])
                    if t == NTT - 1:
                        cp('pool', ztail[:, c, :], zbuf[:, 512:514], ["zmain"], ["ztail%d" % c])
                    else:
                        cp('pool', zbuf[:, 0:2], zbuf[:, 512:514], ["zmain", "zcarry"], ["zcarry"])
            dma('sp', zt_i[li][:, :], ztail[:, :, :].rearrange("p a b -> p (a b)"), "zts",
                ["ztail%d" % c for c in range(4)], ["zti%d" % li])
            S.add('pool',
                  lambda e, li=li: e.collective_compute("AllGather", ALU.bypass, replica_groups=GROUPS,
                                                        ins=[zt_i[li].ap().opt()], outs=[zt_o[li].ap().opt()]),
                  ["zti%d" % li], ["zto%d" % li], dma="cc", inc=1)
            dma('sp', hz[:, :, :].rearrange("p a b -> p (a b)"), zt_o[li][0:128, :], "ztl", ["zto%d" % li], ["hz"])
            ts('dve', hz[:, :, :], hz[:, :, :], flag[:, 0:1], None, ALU.mult, None, ["hz", "flag"], ["hz"])
            for c in range(4):
                tt('dve', corr[:, c, 0:1], hz[:, c, 1:2], cw[:, c, 1:2], ALU.mult, ["hz", "cw"], ["corr%d" % c])
                stt(corr[:, c, 0:1], hz[:, c, 0:1], cw[:, c, 0:1], corr[:, c, 0:1], ALU.mult, ALU.add,
                    ["hz", "cw", "corr%d" % c], ["corr%d" % c])
                tt('dve', corr[:, c, 1:2], hz[:, c, 1:2], cw[:, c, 0:1], ALU.mult, ["hz", "cw", "corr%d" % c], ["corr%d" % c])
                tt('dve', corr[:, c, :], corr[:, c, :], gg01[:, c, :], ALU.mult, ["corr%d" % c, "gg01_%d" % c], ["corr%d" % c])
                tt('dve', R2hi[:, 4 + c, 0:2], R2hi[:, 4 + c, 0:2], corr[:, c, :], ALU.add,
                   ["corr%d" % c, "yc%d_0" % c], ["yc%d_0" % c])

            if dbg == 'B':
                S.fence("dbgf")
                dump(0, q_s[li][0:128, :], dram=True)
                dump(1, kvin[li][0][0:128, :], dram=True)
                dump(2, kvout[li][0][0:128, :], dram=True)
                dump(3, kvin[li][0][128:256, :], dram=True)
                dump(4, ga_s[li][0:128, :], dram=True)
                dump(5, R2hi[:, 0, :])
                dump(6, R2hi[:, 4, :])
                dump(7, R2hi[:, 7, :])
                raise _Stop()
            S.fence("R34gen")
            S.fence("R1gen")
            S.fence("R2gen")
            S.fence("TMPgen")
            for k in range(KC):
                dma('pool', R1[:, k, :], w_out[li, k * 128:(k + 1) * 128, :], "wo", ["R1gen"], ["wo%d" % k])
            memset('pool', Vl[:, :, 128:129], 1.0, ["R34gen"], ["Vl1"])
            memset('pool', Vrr[:, :, 128:129], 1.0, ["R34gen"], ["Vrr1"])
            O1 = T5[0].rearrange("p (i n) -> p i n", i=4)
            Ob = T5[1].rearrange("p (i n) -> p i n", i=4)
            yn = T5[2].rearrange("p (i n) -> p i n", i=4)
            sctr = [0]
            pctr = [0]
            for h in range(NH):
                dma('sp', kTl, kvin[li][h][0:128, :], "kTl", ["kvin%d_%d" % (li, h), "R34gen"], ["kTl"])
                dma('sp', kTr, kvout[li][h][0:128, :], "kTr", ["kvout%d_%d" % (li, h), "R34gen"], ["kTr"])
                dma('sp', qT, q_s[li][h * 128:(h + 1) * 128, :], "qT", ["qs%d_%d" % (li, h), "R34gen"], ["qT"])
                dma('sp', gaT, ga_s[li][h * 128:(h + 1) * 128, :], "gaT", ["gas%d_%d" % (li, h), "R34gen"], ["gaT"])
                srcl = kvin[li][h][128:256, :].rearrange("r (s e) -> (r s) e", e=128).rearrange("(b p) e -> p b e", p=128)
                srcr = kvout[li][h][128:256, :].rearrange("r (s e) -> (r s) e", e=128).rearrange("(b p) e -> p b e", p=128)
                dma('sp', Vl[:, :, 0:128], srcl, "Vl", ["kvin%d_%d" % (li, h), "R34gen"], ["Vl"])
                dma('sp', Vrr[:, :, 0:128], srcr, "Vrr", ["kvout%d_%d" % (li, h), "R34gen"], ["Vrr"])
                ts('pool', Vr[:, :, :], Vrr[:, :, :], flag[:, 0:1], None, ALU.mult, None,
                   ["Vrr", "Vrr1", "flag", "R34gen"], ["Vr"])
                for t in range(NTT):
                    for m in range(2):
                        visits = [('r', j) for j in range(16)] + [('l', j) for j in range(4 * t + 4)]
                        for vi, (src, j) in enumerate(visits):
                            diag = (src == 'l' and j >= 4 * t)
                            r = (j - 4 * t) if diag else 0
                            N = 512 - 128 * r
                            sbk = 4 + (sctr[0] % 3)
                            sctr[0] += 1
                            pbi = pctr[0] % 3
                            pctr[0] += 1
                            kT = kTl if src == 'l' else kTr
                            kk = "kTl" if src == 'l' else "kTr"
                            Vt = Vl if src == 'l' else Vr
                            vk = ["Vl", "Vl1"] if src == 'l' else ["Vr"]
                            mm(ps[sbk][:, 0:N], kT[m * 64:(m + 1) * 64, j * 128:(j + 1) * 128],
                               qT[m * 64:(m + 1) * 64, t * 512 + r * 128:(t + 1) * 512], True, True,
                               [kk, "qT"], ["ps%d" % sbk])
                            act(Pb[pbi][:, 0:N], ps[sbk][:, 0:N], AF.Exp, ["ps%d" % sbk, "R34gen"], ["P%d" % pbi],
                                scale=0.125)
                            if diag:
                                tt('pool', Pb[pbi][:, 0:128], Pb[pbi][:, 0:128], maskb[:, :], ALU.mult,
                                   ["P%d" % pbi, "maskb"], ["P%d" % pbi])
                            for i in range(r, 4):
                                mm(ps[i][:, 0:129], Pb[pbi][:, (i - r) * 128:(i - r + 1) * 128], Vt[:, j, 0:129],
                                   vi == 0, (src == 'l' and j == 4 * t + i), ["P%d" % pbi] + vk, ["ps%d" % i])
                        for i in range(4):
                            if m == 0:
                                recip(stC[:, i:i + 1], ps[i][:, 128:129], ["ps%d" % i], ["rl1_%d" % i])
                                ts('dve', O1[:, i, :], ps[i][:, 0:128], stC[:, i:i + 1], None, ALU.mult, None,
                                   ["ps%d" % i, "rl1_%d" % i, "TMPgen"], ["O1_%d" % i])
                            else:
                                recip(stC[:, 4 + i:5 + i], ps[i][:, 128:129], ["ps%d" % i], ["rl2_%d" % i])
                                tt('dve', stC[:, 8 + i:9 + i], stC[:, 4 + i:5 + i], nlam, ALU.mult,
                                   ["rl2_%d" % i, "nlam"], ["nlr_%d" % i])
                                stt(Ob[:, i, :], ps[i][:, 0:128], stC[:, 8 + i:9 + i], O1[:, i, :], ALU.mult, ALU.add,
                                    ["ps%d" % i, "nlr_%d" % i, "O1_%d" % i, "TMPgen"], ["O_%d" % i])
                    for i in range(4):
                        act(junk2[:, 0:128], Ob[:, i, :], AF.Square, ["O_%d" % i], ["junk2", "ssO%d" % i],
                            accum_out=stC[:, 12 + i:13 + i])
                        act(stC[:, 16 + i:17 + i], stC[:, 12 + i:13 + i], AF.Sqrt, ["ssO%d" % i], ["sdO%d" % i],
                            scale=1.0 / 128, bias=NORM_EPS)
                        recip(stC[:, 20 + i:21 + i], stC[:, 16 + i:17 + i], ["sdO%d" % i], ["rsO%d" % i])
                        stt(yn[:, i, :], Ob[:, i, :], stC[:, 20 + i:21 + i], slnw[:, :], ALU.mult, ALU.mult,
                            ["O_%d" % i, "rsO%d" % i, "slnw", "TMPgen"], ["yn%d" % i])
                        tr(ps[7][:, i * 128:(i + 1) * 128], yn[:, i, :], ["yn%d" % i, "ident"], ["ps7"])
                    tt('dve', ya[:, h, t * 512:(t + 1) * 512], ps[7][:, :], gaT[:, t * 512:(t + 1) * 512], ALU.mult,
                       ["ps7", "gaT", "R2gen"], ["ya%d_%d" % (h, t)])

            if dbg == 'C':
                S.fence("dbgf")
                dump(0, ya[:, 0, :])
                dump(1, ya[:, 7, :])
                dump(2, ya[:, 3, :])
                raise _Stop()
            S.fence("R34gen")
            S.fence("R5gen")
            if do_final:
                dma('sp', nwb, fnwb_in[:, :], "sm", ["R34gen"], ["nwb"])
            ycat_keys = (["ya%d_%d" % (h, t) for h in range(NH) for t in range(NTT)]
                         + ["yb%d_%d" % (g, t) for g in range(4) for t in range(NTT)]
                         + ["yc%d_%d" % (c, t) for c in range(4) for t in range(NTT)])
            for tb in range(NTB):
                s = tb % 2
                t = tb // 4
                xk = ["xb%d_%d" % (s, c) for c in range(4)]
                dma('sp', xb[s], x_src[tb * 128:(tb + 1) * 128, :], "xb%d" % s,
                    ["xrow_%d_%d" % (li, tb), "R5gen"], xk)
                rk = (["ya%d_%d" % (h, t) for h in range(NH)] + ["yb%d_%d" % (g, t) for g in range(4)]
                      + ["yc%d_%d" % (c, t) for c in range(4)])
                for cg in range(4):
                    b = nbank()
                    for k in range(KC):
                        lhs = ya[:, k, tb * 128:(tb + 1) * 128] if k < 8 else R2hi[:, k - 8, tb * 128:(tb + 1) * 128]
                        mm(ps[b][:, :], lhs, R1[:, k, cg * 512:(cg + 1) * 512], k == 0, k == KC - 1,
                           rk + ["wo%d" % k], ["ps%d" % b])
                    tt('dve', xb[s][:, cg * 512:(cg + 1) * 512], ps[b][:, :], xb[s][:, cg * 512:(cg + 1) * 512], ALU.add,
                       ["ps%d" % b, xk[cg]], [xk[cg]])
                if do_final:
                    act(junk, xb[s], AF.Square, xk + ["R34gen"], ["junk", "ssD%d" % tb], accum_out=stA[:, tb:tb + 1])
                    act(stA[:, 16 + tb:17 + tb], stA[:, tb:tb + 1], AF.Sqrt, ["ssD%d" % tb], ["sdD%d" % tb],
                        scale=1.0 / D, bias=NORM_EPS)
                    recip(stA[:, 32 + tb:33 + tb], stA[:, 16 + tb:17 + tb], ["sdD%d" % tb], ["rsD%d" % tb])
                    stt(xb[s], xb[s], stA[:, 32 + tb:33 + tb], nwb, ALU.mult, ALU.mult,
                        xk + ["rsD%d" % tb, "nwb"], xk)
                dma('sp', x_dst[tb * 128:(tb + 1) * 128, :], xb[s], "xst%d" % s, xk,
                    ["xrow_%d_%d" % (li + 1, tb)])

    except _Stop:
        pass

    S.add('sp', lambda e: e.dma_start(out=fsc[:, 0:1], in_=flag_in[:, :]),
          ["xrow_%d_%d" % (L, tb) for tb in range(NTB)] + dbg_keys, ["_endscratch"], dma="end")

    S.resolve(lambda name: es.enter_context(nc.semaphore(name)))
    block = es.enter_context(nc.Block())

    @block.tensor
    def _(e):
        S.emit_engine('pe', e)

    @block.scalar
    def _(e):
        S.emit_engine('act', e)

    @block.vector
    def _(e):
        S.emit_engine('dve', e)

    @block.gpsimd
    def _(e):
        S.emit_engine('pool', e)

    @block.sync
    def _(e):
        S.emit_engine('sp', e)
        last = S.ops[-1]
        e.wait_ge(last['sigval'][0], last['sigval'][1])

    es.close()
    return nc


def _rep(v, n=128):
    return np.ascontiguousarray(np.broadcast_to(np.asarray(v, np.float32).reshape(1, -1), (n, v.size)))


def _host_inputs(layer_ids, x_core, p, w_in, w_out, norm_w, final_norm_w, lam_q1, lam_k1, lam_q2, lam_k2,
                 subln_w, sgu_ln_g, sgu_ln_b, w_s, b_s, conv_w, consts):
    Ls = list(layer_ids)
    m = dict(consts)
    m["x"] = np.ascontiguousarray(x_core, dtype=np.float32)
    m["w_in"] = np.ascontiguousarray(w_in[Ls])
    m["w_out"] = np.ascontiguousarray(w_out[Ls])
    m["nwb"] = np.stack([_rep(norm_w[l]) for l in Ls])
    m["fnwb"] = _rep(final_norm_w)
    m["lamv"] = np.stack([_rep(np.concatenate([lam_q1[l], lam_k1[l], lam_q2[l], lam_k2[l]])) for l in Ls])
    m["slnw"] = np.stack([_rep(subln_w[l]) for l in Ls])
    m["lng"] = np.stack([_rep(sgu_ln_g[l]) for l in Ls])
    m["lnb"] = np.stack([_rep(sgu_ln_b[l]) for l in Ls])
    m["wsT"] = np.stack([np.ascontiguousarray(np.transpose(w_s[l], (2, 0, 1)).reshape(128, 512)) for l in Ls])
    m["bsb"] = np.stack([_rep(b_s[l].reshape(-1)) for l in Ls])
    m["cw"] = np.stack([np.ascontiguousarray(np.transpose(conv_w[l].reshape(3, 4, 128), (2, 1, 0)).reshape(128, 12))
                        for l in Ls])
    m["flag"] = np.full((128, 1), float(p), np.float32)
    return m


def _consts(p):
    pos = (np.arange(T, dtype=np.float32) + np.float32(p * T)).astype(np.float32)
    inv_freq = (np.float32(10000.0) ** (-np.arange(0, 64, 2, dtype=np.float32) / np.float32(64))).astype(np.float32)
    ang = (pos[:, None] * inv_freq[None, :]).astype(np.float32)
    cosT = np.cos(ang).astype(np.float32).T
    sinT = np.sin(ang).astype(np.float32).T
    c = {}
    c["cosT"] = np.ascontiguousarray(np.tile(cosT, (4, 1)))
    c["sinT"] = np.ascontiguousarray(np.tile(sinT, (4, 1)))
    c["ident"] = np.eye(128, dtype=np.float32)
    R = np.zeros((128, 128), np.float32)
    for base in (0, 64):
        for i in range(32):
            R[base + i + 32, base + i] = -1.0
            R[base + i, base + i + 32] = 1.0
    c["rotR"] = R
    kk = np.arange(128)
    c["mask"] = (kk[None, :] >= kk[:, None]).astype(np.float32)
    return c


_NC_CACHE = {}


def _get_nc(layer_ids, final_norm):
    key = (tuple(layer_ids), final_norm)
    if key not in _NC_CACHE:
        _NC_CACHE[key] = build_program(list(layer_ids), final_norm)
    return _NC_CACHE[key]


def _run(layer_ids, final_norm, x_cores, params):
    nc = _get_nc(layer_ids, final_norm)
    in_maps = []
    for c in range(8):
        p = c % 2
        in_maps.append(_host_inputs(layer_ids, x_cores[c], p, consts=_consts(p), **params))
    res = run_bass_kernel_spmd(nc, in_maps, core_ids=list(range(8)))
    return [np.asarray(res.results[c]["out"], dtype=np.float32) for c in range(8)]


FUSED = True


def kernel(x, norm_w, w_in, lam_q1, lam_k1, lam_q2, lam_k2, subln_w, sgu_ln_g, sgu_ln_b,
           w_s, b_s, conv_w, w_out, final_norm_w):
    f = lambda a: np.asarray(a, dtype=np.float32)
    x = f(x)
    params = dict(w_in=f(w_in), w_out=f(w_out), norm_w=f(norm_w), final_norm_w=f(final_norm_w),
                  lam_q1=f(lam_q1), lam_k1=f(lam_k1), lam_q2=f(lam_q2), lam_k2=f(lam_k2),
                  subln_w=f(subln_w), sgu_ln_g=f(sgu_ln_g), sgu_ln_b=f(sgu_ln_b), w_s=f(w_s), b_s=f(b_s),
                  conv_w=f(conv_w))
    x_cores = [x[c // 2, (c % 2) * T:(c % 2 + 1) * T, :] for c in range(8)]
    if FUSED:
        outs = _run([0, 1], True, x_cores, params)
    else:
        mid = _run([0], False, x_cores, params)
        outs = _run([1], True, mid, params)
    out = np.empty((4, 4096, D), np.float32)
    for c in range(8):
        out[c // 2, (c % 2) * T:(c % 2 + 1) * T, :] = outs[c]
    return out
```

```python
import math
import os
from contextlib import ExitStack

import numpy as np
import concourse.bass as bass
import concourse.mybir as mybir
from concourse.bass_utils import run_bass_kernel_spmd

F32 = mybir.dt.float32
BF16 = mybir.dt.bfloat16
AF = mybir.ActivationFunctionType
ALU = mybir.AluOpType
AX = mybir.AxisListType

DEPTH = 2
D = 2048
T = 2048
NTB = 16
NTT = 4
KC = 16
PROJ = 7680
NH = 8
NORM_EPS = 1e-5
LN_EPS = 1e-5
GROUPS = [[0, 1], [2, 3], [4, 5], [6, 7]]
SEM_LIMIT = 6000
PSUM_EXCL = True


class _Stop(Exception):
    pass


class Sched:
    def __init__(self):
        self.ops = []

    def add(self, eng, fn, reads=(), writes=(), dma=None, inc=16):
        self.ops.append(dict(eng=eng, fn=fn, reads=tuple(reads), writes=tuple(writes),
                             dma=dma, inc=inc, deps=(), sig=False, sigval=None, waits=()))

    def fence(self, key):
        self.add('pool', self._fence_fn, reads=(), writes=(key, "_fence_scratch"))
        self.ops[-1]['barrier'] = True

    def resolve(self, new_sem):
        ops = self.ops
        W = {}
        R = {}
        last_eng = {}
        dma_since = []
        last_bar = None
        for i, op in enumerate(ops):
            deps = set()
            if op.get('barrier'):
                deps.update(last_eng.values())
                deps.update(dma_since)
            if last_bar is not None:
                deps.add(last_bar)
            for k in op['reads']:
                w = W.get(k)
                if w:
                    deps.update(w['eng'].values())
                    deps.update(w['dma'])
                if PSUM_EXCL and k.startswith("ps") and k[2:].isdigit():
                    r = R.get(k)
                    if r:
                        deps.update(j for e2, j in r['eng'].items() if e2 != op['eng'])
            for k in op['writes']:
                w = W.get(k)
                if w:
                    deps.update(w['eng'].values())
                    deps.update(w['dma'])
                r = R.get(k)
                if r:
                    deps.update(r['eng'].values())
                    deps.update(r['dma'])
            deps.discard(i)
            keep = []
            for j in deps:
                oj = ops[j]
                if oj['eng'] == 'pe' and op['eng'] == 'pe' and oj['dma'] is None and op['dma'] is None:
                    continue
                keep.append(j)
                oj['sig'] = True
            op['deps'] = keep
            if op.get('barrier'):
                last_bar = i
                dma_since = []
                op['sig'] = True
            if op['dma'] is not None:
                dma_since.append(i)
            else:
                last_eng[op['eng']] = i
            for k in op['reads']:
                r = R.setdefault(k, {'eng': {}, 'dma': []})
                if op['dma'] is not None:
                    r['dma'].append(i)
                else:
                    r['eng'][op['eng']] = i
            for k in op['writes']:
                if op['dma'] is not None:
                    w = W.setdefault(k, {'eng': {}, 'dma': []})
                    w['eng'] = {}
                    w['dma'].append(i)
                else:
                    W[k] = {'eng': {op['eng']: i}, 'dma': []}
                R[k] = {'eng': {}, 'dma': []}
        cnt = {}
        cur = {}
        dcnt = {}
        dsem = {}
        for i, op in enumerate(ops):
            waits = {}
            for j in op['deps']:
                oj = ops[j]
                if oj['dma'] is not None:
                    s, v = dsem[oj['dma']], dcnt[oj['dma']]
                else:
                    s, v = oj['sigval']
                if waits.get(s, (None, 0))[1] < v:
                    waits[s] = (s, v)
            op['waits'] = list(waits.values())
            if op['dma'] is not None:
                k = op['dma']
                if k not in dsem:
                    dsem[k] = new_sem("d_" + k)
                    dcnt[k] = 0
                dcnt[k] += op['inc']
                op['sigval'] = (dsem[k], dcnt[k])
            elif op['sig']:
                e = op['eng']
                if e not in cur or cnt[e] >= SEM_LIMIT:
                    cur[e] = new_sem("c_%s_%d" % (e, i))
                    cnt[e] = 0
                cnt[e] += 1
                op['sigval'] = (cur[e], cnt[e])

    def emit_engine(self, eng, eobj):
        known = {}
        n = 0
        for op in self.ops:
            if op['eng'] != eng:
                continue
            for (s, v) in op['waits']:
                kid = id(s)
                if known.get(kid, 0) >= v:
                    continue
                known[kid] = v
                eobj.wait_ge(s, v)
            ins = op['fn'](eobj)
            n += 1
            if op['dma'] is not None:
                ins.then_inc(op['sigval'][0], op['inc'])
            elif op['sig']:
                ins.then_inc(op['sigval'][0], 1)
        return n


def build_program(layer_ids, final_norm, dbg=None):
    L = len(layer_ids)
    nc = bass.Bass("TRN2", target_bir_lowering=False)
    S = Sched()
    es = ExitStack()

    def din(name, shape, dt=F32):
        return nc.dram_tensor(name, list(shape), dt, kind="ExternalInput")

    x_in = din("x", [T, D])
    w_in = din("w_in", [L, D, PROJ])
    w_out = din("w_out", [L, D, D])
    nwb_in = din("nwb", [L, 128, D])
    fnwb_in = din("fnwb", [128, D])
    cos_in = din("cosT", [128, T])
    sin_in = din("sinT", [128, T])
    lamv_in = din("lamv", [L, 128, 256])
    slnw_in = din("slnw", [L, 128, 128])
    lng_in = din("lng", [L, 128, 512])
    lnb_in = din("lnb", [L, 128, 512])
    wsT_in = din("wsT", [L, 128, 512])
    bsb_in = din("bsb", [L, 128, 512])
    cw_in = din("cw", [L, 128, 12])
    flag_in = din("flag", [128, 1])
    ident_in = din("ident", [128, 128])
    rotR_in = din("rotR", [128, 128])
    mask_in = din("mask", [128, 128])
    out_t = nc.dram_tensor("out", [T, D], F32, kind="ExternalOutput")

    xres = [nc.dram_tensor("xres%d" % i, [T, D], F32) for i in range(max(L - 1, 0))]
    kvin = [[nc.dram_tensor("kvin%d_%d" % (l, h), [256, 2048], BF16) for h in range(NH)] for l in range(L)]
    kvout = [[nc.dram_tensor("kvout%d_%d" % (l, h), [512, 2048], BF16) for h in range(NH)] for l in range(L)]
    q_s = [nc.dram_tensor("qs%d" % l, [1024, 2048], BF16) for l in range(L)]
    ga_s = [nc.dram_tensor("gas%d" % l, [1024, 2048], BF16) for l in range(L)]
    zt_i = [nc.dram_tensor("zti%d" % l, [128, 8], F32) for l in range(L)]
    zt_o = [nc.dram_tensor("zto%d" % l, [256, 8], F32) for l in range(L)]

    def sb(name, shape, dt=F32):
        return es.enter_context(nc.sbuf_tensor(name + "_sb", list(shape), dt))

    R1 = sb("R1", [128, 16, 2048], BF16)
    R2lo = sb("R2lo", [128, 4, 2048], F32)
    R2hi = sb("R2hi", [128, 8, 2048], BF16)
    R34 = sb("R34", [128, 16384], BF16)
    R5 = sb("R5", [128, 4096], F32)
    qbf = [sb("qbf%d" % i, [128, 512], BF16) for i in range(2)]
    vn = sb("vn", [128, 512], BF16)
    ident = sb("ident", [128, 128], F32)
    rotR = sb("rotRb", [128, 128], BF16)
    maskb = sb("maskb", [128, 128], BF16)
    maskf = sb("maskf", [128, 128], F32)
    flag = sb("flagt", [128, 1], F32)
    lamv = sb("lamvt", [128, 256], F32)
    lamt = sb("lamt", [128, 8], F32)
    slnw = sb("slnwt", [128, 128], F32)
    wsTb = sb("wsTb", [128, 4, 128], BF16)
    bsb = sb("bsbt", [128, 4, 128], F32)
    cw = sb("cwt", [128, 4, 3], F32)
    stA = sb("stA", [128, 48], F32)
    stB = sb("stB", [128, 8], F32)
    stC = sb("stC", [128, 32], F32)
    gg01 = sb("gg01", [128, 4, 2], F32)
    ztail = sb("ztail", [128, 4, 2], F32)
    hz = sb("hz", [128, 4, 2], F32)
    corr = sb("corr", [128, 4, 2], F32)
    fsc = sb("fsc", [128, 1], F32)
    junk2 = sb("junk2", [128, 512], BF16)

    ps = [es.enter_context(nc.psum_tensor("ps%d" % i, [128, 512], F32)) for i in range(8)]

    S._fence_fn = lambda e: e.memset(fsc[:, 0:1], 0.0)

    ya = R2lo[:, :, :].bitcast(BF16).rearrange("p g (two n) -> p (g two) n", two=2)
    stg = [ya[:, 0, :], ya[:, 1, :]]
    vst = [ya[:, 2, :].rearrange("p (b n) -> p b n", b=4), ya[:, 3, :].rearrange("p (b n) -> p b n", b=4)]
    wbuf = [R34[:, s * 8192:(s + 1) * 8192].rearrange("p (k n) -> p k n", k=16) for s in range(2)]
    nwb = R34[:, 8192:12288].bitcast(F32)
    junk = R34[:, 12288:14336]
    kTl = R34[:, 0:2048]
    kTr = R34[:, 2048:4096]
    qT = R34[:, 4096:6144]
    gaT = R34[:, 6144:8192]
    Vl = R34[:, 8192:8192 + 2064].rearrange("p (b n) -> p b n", b=16)
    Vrr = R34[:, 10256:10256 + 2064].rearrange("p (b n) -> p b n", b=16)
    Vr = R34[:, 12320:12320 + 2064].rearrange("p (b n) -> p b n", b=16)
    Pb = [R34[:, 14384 + i * 512:14384 + (i + 1) * 512] for i in range(3)]
    xb = [R5[:, 0:2048], R5[:, 2048:4096]]
    T5 = [R5[:, i * 512:(i + 1) * 512] for i in range(8)]
    TR = [R2lo[:, 2, i * 512:(i + 1) * 512] for i in range(4)]
    lng, lnb = T5[6], T5[7]
    wsTf = R34[:, 0:1024].bitcast(F32).rearrange("p (g n) -> p g n", g=4)
    cosT, sinT = xb[0], xb[1]

    def dma(eng, out, in_, key, reads, writes):
        S.add(eng, lambda e: e.dma_start(out=out, in_=in_), reads, writes, dma=key)

    def mm(out, lhsT, rhs, start, stop, reads, writes):
        S.add('pe', lambda e: e.matmul(out, lhsT, rhs, start=start, stop=stop), reads, writes)

    def tr(out, in_, reads, writes):
        S.add('pe', lambda e: e.transpose(out, in_, ident[:, :]), reads, writes)

    def act(out, in_, func, reads, writes, bias=None, scale=None, accum_out=None):
        kw = {}
        if bias is not None:
            kw['bias'] = bias
        if scale is not None:
            kw['scale'] = scale
        if accum_out is not None:
            kw['accum_out'] = accum_out
        S.add('act', lambda e: e.activation(out, in_, func, **kw), reads, writes)

    def tt(eng, out, in0, in1, op, reads, writes):
        S.add(eng, lambda e: e.tensor_tensor(out, in0, in1, op), reads, writes)

    def ts(eng, out, in0, s1, s2, op0, op1, reads, writes):
        if op1 is None:
            S.add(eng, lambda e: e.tensor_scalar(out, in0, s1, None, op0), reads, writes)
        else:
            S.add(eng, lambda e: e.tensor_scalar(out, in0, s1, s2, op0, op1), reads, writes)

    def stt(out, in0, scalar, in1, op0, op1, reads, writes):
        S.add('dve', lambda e: e.scalar_tensor_tensor(out, in0, scalar, in1, op0, op1), reads, writes)

    def recip(out, in_, reads, writes):
        S.add('dve', lambda e: e.reciprocal(out, in_), reads, writes)

    def cp(eng, out, in_, reads, writes):
        S.add(eng, lambda e: e.tensor_copy(out, in_), reads, writes)

    def memset(eng, ap, val, reads, writes):
        S.add(eng, lambda e: e.memset(ap, val), reads, writes)

    bank_ctr = [0]

    def nbank():
        b = bank_ctr[0] % 8
        bank_ctr[0] += 1
        return b

    dbg_t = nc.dram_tensor("dbg", [128, 8 * 2048], F32, kind="ExternalOutput") if dbg else None
    dbg_keys = []

    def dump(slot, src_ap, dram=False):
        s2 = slot % 2
        if dram:
            dma('sp', R34[:, 0:2048], src_ap, "dstage", [], ["dstage"])
            src_ap = R34[:, 0:2048]
        cp('dve', xb[s2], src_ap, ["dstage"], ["xbd%d" % s2])
        dma('sp', dbg_t[:, slot * 2048:(slot + 1) * 2048], xb[s2], "dbg%d" % s2, ["xbd%d" % s2], ["dbgout%d" % slot])
        dbg_keys.append("dbgout%d" % slot)

    dma('sp', ident[:, :], ident_in[:, :], "cst", [], ["ident"])
    dma('sp', maskf[:, :], mask_in[:, :], "cst", [], ["maskf"])
    dma('sp', flag[:, :], flag_in[:, :], "cst", [], ["flag"])
    dma('pool', rotR[:, :], rotR_in[:, :], "cstp", [], ["rotR"])
    dma('pool', maskb[:, :], mask_in[:, :], "cstp", [], ["maskb"])

    hT_keys = ["hT%d_%d" % (tb, q4) for tb in range(NTB) for q4 in range(4)]

    try:
        for li, layer in enumerate(layer_ids):
            lam_init = 0.8 - 0.6 * math.exp(-0.3 * layer)
            last = (li == L - 1)
            x_src = x_in if li == 0 else xres[li - 1]
            x_dst = out_t if last else xres[li]
            do_final = last and final_norm
            LK = "L%d" % li

            def xkey(t, tb):
                return "dram_%s_%d" % (t.name if hasattr(t, "name") else id(t), tb)

            S.fence("R34gen")
            dma('sp', nwb, nwb_in[li, :, :], "sm", ["R34gen"], ["nwb"])
            dma('sp', lamv[:, :], lamv_in[li, :, :], "sm", [], ["lamv"])
            dma('sp', slnw[:, :], slnw_in[li, :, :], "sm", [], ["slnw"])
            dma('sp', wsTf, wsT_in[li, :, :].rearrange("p (g n) -> p g n", g=4), "sm", [], ["wsTf"])
            dma('sp', bsb[:, :, :], bsb_in[li, :, :].rearrange("p (g n) -> p g n", g=4), "sm", [], ["bsb"])
            dma('sp', cw[:, :, :], cw_in[li, :, :].rearrange("p (g n) -> p g n", g=4), "sm", [], ["cw"])
            nlam = lamt[:, 5:6]
            _sk = os.environ.get('SKIPSM', '0')
            if True:
                if _sk not in ('1', '2'):
                    tt('dve', lamv[:, 0:64], lamv[:, 0:64], lamv[:, 64:128], ALU.mult, ["lamv"], ["lamp1"])
                    tt('dve', lamv[:, 128:192], lamv[:, 128:192], lamv[:, 192:256], ALU.mult, ["lamv"], ["lamp2"])
                    if _sk != '4':
                        act(junk2[:, 0:64], lamv[:, 0:64], AF.Copy, ["lamp1"], ["junk2", "lams1"], accum_out=lamt[:, 0:1])
                        act(junk2[:, 0:64], lamv[:, 128:192], AF.Copy, ["lamp2"], ["junk2", "lams2"], accum_out=lamt[:, 1:2])
                        act(lamt[:, 2:3], lamt[:, 0:1], AF.Exp, ["lams1"], ["lame1"])
                        act(lamt[:, 3:4], lamt[:, 1:2], AF.Exp, ["lams2"], ["lame2"])
                    if _sk not in ('4', '5'):
                        tt('dve', lamt[:, 4:5], lamt[:, 3:4], lamt[:, 2:3], ALU.subtract, ["lame1", "lame2"], ["lamd"])
                        ts('dve', lamt[:, 5:6], lamt[:, 4:5], -lam_init, None, ALU.add, None, ["lamd"], ["nlam"])
                nlam = lamt[:, 5:6]
                if _sk not in ('1', '3'):
                    for g in range(4):
                        tt('dve', wsTb[:, g, :], wsTf[:, g, :], maskf[:, :], ALU.mult, ["wsTf", "maskf"], ["wsTb%d" % g])
                    ts('dve', slnw[:, :], slnw[:, :], 1.0 - lam_init, None, ALU.mult, None, ["slnw"], ["slnw"])


            S.fence("R1gen")
            S.fence("R5gen")
            for tb in range(0 if os.environ.get('SKIPA') else NTB):
                s = tb % 2
                xk = ["xb%d_%d" % (s, c) for c in range(4)]
                dma('sp', xb[s], x_src[tb * 128:(tb + 1) * 128, :], "xb%d" % s,
                    ["xrow_%d_%d" % (li, tb), "R5gen"], xk)
                act(junk, xb[s], AF.Square, xk + ["R34gen"], ["junk", "ssA%d" % tb], accum_out=stA[:, tb:tb + 1])
                act(stA[:, 16 + tb:17 + tb], stA[:, tb:tb + 1], AF.Sqrt, ["ssA%d" % tb], ["sdA%d" % tb],
                    scale=1.0 / D, bias=NORM_EPS)
                recip(stA[:, 32 + tb:33 + tb], stA[:, 16 + tb:17 + tb], ["sdA%d" % tb], ["rsA%d" % tb])
                stt(xb[s], xb[s], stA[:, 32 + tb:33 + tb], nwb, ALU.mult, ALU.mult,
                    xk + ["rsA%d" % tb, "nwb"], xk)
                for q4 in range(4):
                    b = nbank()
                    for j in range(4):
                        c = q4 * 4 + j
                        tr(ps[b][:, j * 128:(j + 1) * 128], xb[s][:, c * 128:(c + 1) * 128],
                           xk + ["ident"], ["ps%d" % b])
                    eng = 'dve' if q4 % 2 == 0 else 'act'
                    outv = R1[:, q4 * 4:(q4 + 1) * 4, tb * 128:(tb + 1) * 128]
                    inv = ps[b][:, :].rearrange("p (j n) -> p j n", j=4)
                    if eng == 'dve':
                        cp('dve', outv, inv, ["ps%d" % b, "R1gen"], ["hT%d_%d" % (tb, q4)])
                    else:
                        act(outv, inv, AF.Copy, ["ps%d" % b, "R1gen"], ["hT%d_%d" % (tb, q4)])

            if dbg == 'A':
                S.fence("dbgf")
                dump(0, R1[:, 0, :])
                dump(1, R1[:, 15, :])
                raise _Stop()
            S.fence("R34gen")
            S.fence("R5gen")
            S.fence("R2gen")
            S.fence("TMPgen")
            dma('sp', cosT, cos_in[:, :], "cs", ["R5gen"], ["cosT"])
            dma('sp', sinT, sin_in[:, :], "cs", ["R5gen"], ["sinT"])
            wslot = [0]

            def load_w(blocks):
                s = wslot[0] % 2
                wslot[0] += 1
                off = 0
                for (c0, n) in blocks:
                    for kh in range(2):
                        dma('pool', wbuf[s][:, kh * 8:(kh + 1) * 8, off:off + n],
                            w_in[li, kh * 1024:(kh + 1) * 1024, c0:c0 + n].rearrange("(k p) n -> p k n", p=128),
                            "w%d" % s, ["R34gen"], ["w%d" % s])
                    off += n
                return s

            def fm_block(s, cb, t):
                b = nbank()
                rk = ["w%d" % s] + ["hT%d_%d" % (tb, q) for tb in range(4 * t, 4 * t + 4) for q in range(4)]
                for k in range(KC):
                    mm(ps[b][:, :], wbuf[s][:, k, cb * 128:(cb + 1) * 128], R1[:, k, t * 512:(t + 1) * 512],
                       k == 0, k == KC - 1, rk, ["ps%d" % b])
                return b

            def tm_block(s, tb):
                b = nbank()
                rk = ["w%d" % s] + ["hT%d_%d" % (tb, q) for q in range(4)]
                for k in range(KC):
                    mm(ps[b][:, :], R1[:, k, tb * 128:(tb + 1) * 128], wbuf[s][:, k, 0:512],
                       k == 0, k == KC - 1, rk, ["ps%d" % b])
                return b

            stg_ctr = [0]
            dbg_it = [0]

            def rope_job(col0, dst_fn, dkey_fn):
                s = load_w([(col0, 512)])
                dk = os.environ.get("DBGK")
                dk_n, dk_lv = (int(dk.split(",")[0]), int(dk.split(",")[1])) if dk else (None, 9)
                for cb in range(4):
                    sg = stg_ctr[0] % 2
                    stg_ctr[0] += 1
                    for t in range(NTT):
                        if dk_n is not None and dbg_it[0] >= dk_n:
                            S.fence("dbgf")
                            dump(0, stg[0])
                            raise _Stop()
                        dbg_it[0] += 1
                        b = fm_block(s, cb, t)
                        qi = t % 2
                        if dk_lv == 0:
                            act(stg[sg][:, t * 512:(t + 1) * 512], ps[b][:, :], AF.Copy, ["ps%d" % b], ["stg%d_%d" % (sg, t)])
                            continue
                        act(qbf[qi][:, :], ps[b][:, :], AF.Copy, ["ps%d" % b], ["qbf%d" % qi])
                        b2 = nbank()
                        mm(ps[b2][:, :], rotR[:, :], qbf[qi][:, :], True, True, ["rotR", "qbf%d" % qi], ["ps%d" % b2])
                        t1 = TR[2 * qi]
                        t2 = TR[2 * qi + 1]
                        if dk_lv == 1:
                            act(stg[sg][:, t * 512:(t + 1) * 512], ps[b2][:, :], AF.Copy, ["ps%d" % b2], ["stg%d_%d" % (sg, t)])
                            continue
                        if dk_lv == 6:
                            cp('dve', t1, ps[b][:, :], ["ps%d" % b], ["T%d" % (2 * qi)])
                            act(stg[sg][:, t * 512:(t + 1) * 512], ps[b][:, :], AF.Copy, ["ps%d" % b], ["stg%d_%d" % (sg, t)])
                            continue
                        if dk_lv == 7:
                            act(stg[sg][:, t * 512:(t + 1) * 512], t1, AF.Copy, ["ps%d" % b], ["stg%d_%d" % (sg, t)])
                            continue
                        if dk_lv in (2, 4, 5):
                            if dk_lv == 2:
                                tt('dve', t1, ps[b][:, :], cosT[:, t * 512:(t + 1) * 512], ALU.mult,
                                   ["ps%d" % b, "cosT", "TMPgen"], ["T%d" % (2 * qi)])
                            elif dk_lv == 4:
                                tt('dve', t1, ps[b][:, :], ps[b][:, :], ALU.mult, ["ps%d" % b], ["T%d" % (2 * qi)])
                            else:
                                cp('dve', t1, ps[b][:, :], ["ps%d" % b], ["T%d" % (2 * qi)])
                            act(stg[sg][:, t * 512:(t + 1) * 512], t1, AF.Copy, ["T%d" % (2 * qi)], ["stg%d_%d" % (sg, t)])
                            continue
                        tt('dve', t1, ps[b][:, :], cosT[:, t * 512:(t + 1) * 512], ALU.mult,
                           ["ps%d" % b, "cosT", "TMPgen"], ["T%d" % (2 * qi)])
                        tt('dve', t2, ps[b2][:, :], sinT[:, t * 512:(t + 1) * 512], ALU.mult,
                           ["ps%d" % b2, "sinT", "TMPgen"], ["T%d" % (2 * qi + 1)])
                        tt('pool', stg[sg][:, t * 512:(t + 1) * 512], t1, t2, ALU.add,
                           ["T%d" % (2 * qi), "T%d" % (2 * qi + 1), "R2gen"], ["stg%d_%d" % (sg, t)])
                    dma('sp', dst_fn(cb), stg[sg], "stg%d" % sg, ["stg%d_%d" % (sg, t) for t in range(4)],
                        [dkey_fn(cb)])

            for hh in range(2):
                rope_job(1024 + hh * 512,
                         lambda cb, hh=hh: kvin[li][hh * 4 + cb][0:128, :],
                         lambda cb, hh=hh: "kvin%d_%d" % (li, hh * 4 + cb))
            if dbg == 'B1':
                S.fence("dbgf")
                dump(1, kvin[li][0][0:128, :], dram=True)
                raise _Stop()
            vst_ctr = [0]
            for hh in range(2):
                s = load_w([(2048 + hh * 512, 512)])
                for tq in range(4):
                    vs_ = vst_ctr[0] % 2
                    vst_ctr[0] += 1
                    for j in range(4):
                        tb = tq * 4 + j
                        b = tm_block(s, tb)
                        eng = 'dve' if j % 2 == 0 else 'act'
                        if eng == 'dve':
                            cp('dve', vst[vs_][:, j, :], ps[b][:, :], ["ps%d" % b, "R2gen"], ["vst%d_%d" % (vs_, j)])
                        else:
                            act(vst[vs_][:, j, :], ps[b][:, :], AF.Copy, ["ps%d" % b, "R2gen"], ["vst%d_%d" % (vs_, j)])
                    for cb in range(4):
                        h = hh * 4 + cb
                        dstv = kvin[li][h][128:256, :].rearrange("r (s e) -> (r s) e", e=128)
                        dstv = dstv[tq * 512:(tq + 1) * 512, :].rearrange("(b p) e -> p b e", p=128)
                        dma('sp', dstv, vst[vs_][:, :, cb * 128:(cb + 1) * 128], "vst%d" % vs_,
                            ["vst%d_%d" % (vs_, j) for j in range(4)], ["kvin%d_%d" % (li, h)])
            for h in range(NH):
                S.add('pool',
                      lambda e, h=h, li=li: e.collective_compute("AllGather", ALU.bypass, replica_groups=GROUPS,
                                                                 ins=[kvin[li][h].ap().opt()],
                                                                 outs=[kvout[li][h].ap().opt()]),
                      ["kvin%d_%d" % (li, h)], ["kvout%d_%d" % (li, h)], dma="cc", inc=1)
            if dbg == 'B2':
                S.fence("dbgf")
                dump(1, kvin[li][0][0:128, :], dram=True)
                dump(2, kvout[li][0][0:128, :], dram=True)
                dump(3, kvin[li][0][128:256, :], dram=True)
                raise _Stop()
            for hh in range(2):
                rope_job(hh * 512,
                         lambda cb, hh=hh: q_s[li][(hh * 4 + cb) * 128:(hh * 4 + cb + 1) * 128, :],
                         lambda cb, hh=hh: "qs%d_%d" % (li, hh * 4 + cb))
            for hh in range(2):
                s = load_w([(3072 + hh * 512, 512)])
                for cb in range(4):
                    h = hh * 4 + cb
                    sg = stg_ctr[0] % 2
                    stg_ctr[0] += 1
                    for t in range(NTT):
                        b = fm_block(s, cb, t)
                        act(stg[sg][:, t * 512:(t + 1) * 512], ps[b][:, :], AF.Silu, ["ps%d" % b, "R2gen"],
                            ["stg%d_%d" % (sg, t)])
                    dma('sp', ga_s[li][h * 128:(h + 1) * 128, :], stg[sg], "stg%d" % sg,
                        ["stg%d_%d" % (sg, t) for t in range(4)], ["gas%d_%d" % (li, h)])
            if dbg == 'B3':
                S.fence("dbgf")
                dump(0, q_s[li][0:128, :], dram=True)
                dump(1, kvin[li][0][0:128, :], dram=True)
                dump(2, kvout[li][0][0:128, :], dram=True)
                dump(3, kvin[li][0][128:256, :], dram=True)
                dump(4, ga_s[li][0:128, :], dram=True)
                raise _Stop()
            S.fence("R2gen")
            S.fence("TMPgen")
            mixb = R2lo
            dma('sp', lng, lng_in[li, :, :], "sm2", [], ["lng"])
            dma('sp', lnb, lnb_in[li, :, :], "sm2", [], ["lnb"])
            s = load_w([(4608, 512)])
            for tb in range(NTB):
                b = tm_block(s, tb)
                pk = ["ps%d" % b]
                act(junk2[:, :], ps[b][:, :], AF.Copy, pk, ["junk2", "lnsm"], accum_out=stB[:, 0:1])
                act(junk2[:, :], ps[b][:, :], AF.Square, pk, ["junk2", "lnsq"], accum_out=stB[:, 1:2])
                ts('dve', stB[:, 2:3], stB[:, 0:1], 1.0 / 512, None, ALU.mult, None, ["lnsm"], ["lnmean"])
                tt('dve', stB[:, 3:4], stB[:, 2:3], stB[:, 2:3], ALU.mult, ["lnmean"], ["lnmsq"])
                stt(stB[:, 4:5], stB[:, 1:2], 1.0 / 512, stB[:, 3:4], ALU.mult, ALU.subtract,
                    ["lnsq", "lnmsq"], ["lnvar"])
                act(stB[:, 5:6], stB[:, 4:5], AF.Sqrt, ["lnvar"], ["lnsd"], bias=LN_EPS)
                recip(stB[:, 6:7], stB[:, 5:6], ["lnsd"], ["lnrs"])
                ts('dve', T5[0], ps[b][:, :], stB[:, 2:3], stB[:, 6:7], ALU.subtract, ALU.mult,
                   pk + ["lnmean", "lnrs", "TMPgen"], ["T0"])
                tt('pool', T5[1], T5[0], lng, ALU.mult, ["T0", "lng", "TMPgen"], ["T1"])
                tt('pool', vn[:, :], T5[1], lnb, ALU.add, ["T1", "lnb"], ["vn"])
                b2 = nbank()
                for g in range(4):
                    mm(ps[b2][:, g * 128:(g + 1) * 128], vn[:, g * 128:(g + 1) * 128], wsTb[:, g, :], True, True,
                       ["vn", "wsTb%d" % g], ["ps%d" % b2])
                tt('dve', mixb[:, :, tb * 128:(tb + 1) * 128], ps[b2][:, :].rearrange("p (g n) -> p g n", g=4),
                   bsb[:, :, :], ALU.add, ["ps%d" % b2, "bsb", "R2gen"], ["mixb%d" % tb])
            for gp in range(2):
                s = load_w([(4096 + (2 * gp) * 128, 128), (5120 + (2 * gp) * 128, 128),
                            (4096 + (2 * gp + 1) * 128, 128), (5120 + (2 * gp + 1) * 128, 128)])
                for gi in range(2):
                    g = 2 * gp + gi
                    for t in range(NTT):
                        bu = fm_block(s, 2 * gi, t)
                        bg = fm_block(s, 2 * gi + 1, t)
                        act(T5[2], ps[bg][:, :], AF.Silu, ["ps%d" % bg, "TMPgen"], ["T2"])
                        tt('dve', T5[3], ps[bu][:, :], mixb[:, g, t * 512:(t + 1) * 512], ALU.mult,
                           ["ps%d" % bu, "TMPgen"] + ["mixb%d" % tb for tb in range(4 * t, 4 * t + 4)], ["T3"])
                        tt('pool', R2hi[:, g, t * 512:(t + 1) * 512], T5[2], T5[3], ALU.mult,
                           ["T2", "T3"], ["yb%d_%d" % (g, t)])
            if dbg == 'B4':
                S.fence("dbgf")
                dump(0, q_s[li][0:128, :], dram=True)
                dump(1, kvin[li][0][0:128, :], dram=True)
                dump(2, kvout[li][0][0:128, :], dram=True)
                dump(3, kvin[li][0][128:256, :], dram=True)
                dump(4, ga_s[li][0:128, :], dram=True)
                dump(5, R2hi[:, 0, :])
                raise _Stop()
            S.fence("TMPgen")
            zbuf = R5[:, 2048:3072]
            for c in range(4):
                s = load_w([(5632 + c * 128, 128), (6656 + c * 128, 128), (6144 + c * 128, 128), (7168 + c * 128, 128)])
                memset('pool', zbuf[:, 0:2], 0.0, ["TMPgen", "zcarry"], ["zcarry"])
                for t in range(NTT):
                    bx = fm_block(s, 0, t)
                    bc = fm_block(s, 1, t)
                    bb = fm_block(s, 2, t)
                    bgc = fm_block(s, 3, t)
                    act(T5[0], ps[bx][:, :], AF.Copy, ["ps%d" % bx, "TMPgen"], ["T0"])
                    tt('dve', zbuf[:, 2:514], ps[bc][:, :], T5[0], ALU.mult, ["ps%d" % bc, "T0", "TMPgen"], ["zmain"])
                    ts('dve', T5[1], zbuf[:, 2:514], cw[:, c, 2:3], None, ALU.mult, None,
                       ["zmain", "cw", "TMPgen"], ["T1"])
                    stt(T5[1], zbuf[:, 1:513], cw[:, c, 1:2], T5[1], ALU.mult, ALU.add,
                        ["zmain", "zcarry", "cw", "T1"], ["T1"])
                    stt(T5[1], zbuf[:, 0:512], cw[:, c, 0:1], T5[1], ALU.mult, ALU.add,
                        ["zmain", "zcarry", "cw", "T1"], ["T1"])
                    act(T5[2], ps[bgc][:, :], AF.Silu, ["ps%d" % bgc, "TMPgen"], ["T2"])
                    tt('dve', T5[3], ps[bb][:, :], T5[2], ALU.mult, ["ps%d" % bb, "T2", "TMPgen"], ["T3"])
                    tt('pool', R2hi[:, 4 + c, t * 512:(t + 1) * 512], T5[1], T5[3], ALU.mult,
                       ["T1", "T3"], ["yc%d_%d" % (c, t)])
                    if t == 0:
                        cp('pool', gg01[:, c, :], T5[3][:, 0:2], ["T3"], ["gg01_%d" % c])
                    if t == NTT - 1:
                        cp('pool', ztail[:, c, :], zbuf[:, 512:514], ["zmain"], ["ztail%d" % c])
                    else:
                        cp('pool', zbuf[:, 0:2], zbuf[:, 512:514], ["zmain", "zcarry"], ["zcarry"])
            dma('sp', zt_i[li][:, :], ztail[:, :, :].rearrange("p a b -> p (a b)"), "zts",
                ["ztail%d" % c for c in range(4)], ["zti%d" % li])
            S.add('pool',
                  lambda e, li=li: e.collective_compute("AllGather", ALU.bypass, replica_groups=GROUPS,
                                                        ins=[zt_i[li].ap().opt()], outs=[zt_o[li].ap().opt()]),
                  ["zti%d" % li], ["zto%d" % li], dma="cc", inc=1)
            dma('sp', hz[:, :, :].rearrange("p a b -> p (a b)"), zt_o[li][0:128, :], "ztl", ["zto%d" % li], ["hz"])
            ts('dve', hz[:, :, :], hz[:, :, :], flag[:, 0:1], None, ALU.mult, None, ["hz", "flag"], ["hz"])
            for c in range(4):
                tt('dve', corr[:, c, 0:1], hz[:, c, 1:2], cw[:, c, 1:2], ALU.mult, ["hz", "cw"], ["corr%d" % c])
                stt(corr[:, c, 0:1], hz[:, c, 0:1], cw[:, c, 0:1], corr[:, c, 0:1], ALU.mult, ALU.add,
                    ["hz", "cw", "corr%d" % c], ["corr%d" % c])
                tt('dve', corr[:, c, 1:2], hz[:, c, 1:2], cw[:, c, 0:1], ALU.mult, ["hz", "cw", "corr%d" % c], ["corr%d" % c])
                tt('dve', corr[:, c, :], corr[:, c, :], gg01[:, c, :], ALU.mult, ["corr%d" % c, "gg01_%d" % c], ["corr%d" % c])
                tt('dve', R2hi[:, 4 + c, 0:2], R2hi[:, 4 + c, 0:2], corr[:, c, :], ALU.add,
                   ["corr%d" % c, "yc%d_0" % c], ["yc%d_0" % c])

            if dbg == 'B':
                S.fence("dbgf")
                dump(0, q_s[li][0:128, :], dram=True)
                dump(1, kvin[li][0][0:128, :], dram=True)
                dump(2, kvout[li][0][0:128, :], dram=True)
                dump(3, kvin[li][0][128:256, :], dram=True)
                dump(4, ga_s[li][0:128, :], dram=True)
                dump(5, R2hi[:, 0, :])
                dump(6, R2hi[:, 4, :])
                dump(7, R2hi[:, 7, :])
                raise _Stop()
            S.fence("R34gen")
            S.fence("R1gen")
            S.fence("R2gen")
            S.fence("TMPgen")
            for k in range(KC):
                dma('pool', R1[:, k, :], w_out[li, k * 128:(k + 1) * 128, :], "wo", ["R1gen"], ["wo%d" % k])
            memset('pool', Vl[:, :, 128:129], 1.0, ["R34gen"], ["Vl1"])
            memset('pool', Vrr[:, :, 128:129], 1.0, ["R34gen"], ["Vrr1"])
            O1 = T5[0].rearrange("p (i n) -> p i n", i=4)
            Ob = T5[1].rearrange("p (i n) -> p i n", i=4)
            yn = T5[2].rearrange("p (i n) -> p i n", i=4)
            sctr = [0]
            pctr = [0]
            for h in range(NH):
                dma('sp', kTl, kvin[li][h][0:128, :], "kTl", ["kvin%d_%d" % (li, h), "R34gen"], ["kTl"])
                dma('sp', kTr, kvout[li][h][0:128, :], "kTr", ["kvout%d_%d" % (li, h), "R34gen"], ["kTr"])
                dma('sp', qT, q_s[li][h * 128:(h + 1) * 128, :], "qT", ["qs%d_%d" % (li, h), "R34gen"], ["qT"])
                dma('sp', gaT, ga_s[li][h * 128:(h + 1) * 128, :], "gaT", ["gas%d_%d" % (li, h), "R34gen"], ["gaT"])
                srcl = kvin[li][h][128:256, :].rearrange("r (s e) -> (r s) e", e=128).rearrange("(b p) e -> p b e", p=128)
                srcr = kvout[li][h][128:256, :].rearrange("r (s e) -> (r s) e", e=128).rearrange("(b p) e -> p b e", p=128)
                dma('sp', Vl[:, :, 0:128], srcl, "Vl", ["kvin%d_%d" % (li, h), "R34gen"], ["Vl"])
                dma('sp', Vrr[:, :, 0:128], srcr, "Vrr", ["kvout%d_%d" % (li, h), "R34gen"], ["Vrr"])
                ts('pool', Vr[:, :, :], Vrr[:, :, :], flag[:, 0:1], None, ALU.mult, None,
                   ["Vrr", "Vrr1", "flag", "R34gen"], ["Vr"])
                for t in range(NTT):
                    for m in range(2):
                        visits = [('r', j) for j in range(16)] + [('l', j) for j in range(4 * t + 4)]
                        LOOK = 2

                        def emit_qk(vi, src, j):
                            diag = (src == 'l' and j >= 4 * t)
                            r = (j - 4 * t) if diag else 0
                            N = 512 - 128 * r
                            sbk = 4 + (sctr[0] % 3)
                            sctr[0] += 1
                            pbi = pctr[0] % 3
                            pctr[0] += 1
                            kT = kTl if src == 'l' else kTr
                            kk = "kTl" if src == 'l' else "kTr"
                            mm(ps[sbk][:, 0:N], kT[m * 64:(m + 1) * 64, j * 128:(j + 1) * 128],
                               qT[m * 64:(m + 1) * 64, t * 512 + r * 128:(t + 1) * 512], True, True,
                               [kk, "qT"], ["ps%d" % sbk])
                            act(Pb[pbi][:, 0:N], ps[sbk][:, 0:N], AF.Exp, ["ps%d" % sbk, "R34gen"], ["P%d" % pbi],
                                scale=0.125)
                            if diag:
                                tt('pool', Pb[pbi][:, 0:128], Pb[pbi][:, 0:128], maskb[:, :], ALU.mult,
                                   ["P%d" % pbi, "maskb"], ["P%d" % pbi])
                            return (r, pbi)

                        def emit_pv(vi, src, j, r, pbi):
                            Vt = Vl if src == 'l' else Vr
                            vk = ["Vl", "Vl1"] if src == 'l' else ["Vr"]
                            for i in range(r, 4):
                                mm(ps[i][:, 0:129], Pb[pbi][:, (i - r) * 128:(i - r + 1) * 128], Vt[:, j, 0:129],
                                   vi == 0, (src == 'l' and j == 4 * t + i), ["P%d" % pbi] + vk, ["ps%d" % i])

                        pend = {}
                        nv = len(visits)
                        for vi in range(min(LOOK, nv)):
                            pend[vi] = emit_qk(vi, *visits[vi])
                        for vi in range(nv):
                            if vi + LOOK < nv:
                                pend[vi + LOOK] = emit_qk(vi + LOOK, *visits[vi + LOOK])
                            r_, pbi_ = pend.pop(vi)
                            emit_pv(vi, visits[vi][0], visits[vi][1], r_, pbi_)
                        for i in range(4):
                            if m == 0:
                                recip(stC[:, i:i + 1], ps[i][:, 128:129], ["ps%d" % i], ["rl1_%d" % i])
                                ts('dve', O1[:, i, :], ps[i][:, 0:128], stC[:, i:i + 1], None, ALU.mult, None,
                                   ["ps%d" % i, "rl1_%d" % i, "TMPgen"], ["O1_%d" % i])
                            else:
                                recip(stC[:, 4 + i:5 + i], ps[i][:, 128:129], ["ps%d" % i], ["rl2_%d" % i])
                                tt('dve', stC[:, 8 + i:9 + i], stC[:, 4 + i:5 + i], nlam, ALU.mult,
                                   ["rl2_%d" % i, "nlam"], ["nlr_%d" % i])
                                stt(Ob[:, i, :], ps[i][:, 0:128], stC[:, 8 + i:9 + i], O1[:, i, :], ALU.mult, ALU.add,
                                    ["ps%d" % i, "nlr_%d" % i, "O1_%d" % i, "TMPgen"], ["O_%d" % i])
                    for i in range(4):
                        act(junk2[:, 0:128], Ob[:, i, :], AF.Square, ["O_%d" % i], ["junk2", "ssO%d" % i],
                            accum_out=stC[:, 12 + i:13 + i])
                        act(stC[:, 16 + i:17 + i], stC[:, 12 + i:13 + i], AF.Sqrt, ["ssO%d" % i], ["sdO%d" % i],
                            scale=1.0 / 128, bias=NORM_EPS)
                        recip(stC[:, 20 + i:21 + i], stC[:, 16 + i:17 + i], ["sdO%d" % i], ["rsO%d" % i])
                        stt(yn[:, i, :], Ob[:, i, :], stC[:, 20 + i:21 + i], slnw[:, :], ALU.mult, ALU.mult,
                            ["O_%d" % i, "rsO%d" % i, "slnw", "TMPgen"], ["yn%d" % i])
                        tr(ps[7][:, i * 128:(i + 1) * 128], yn[:, i, :], ["yn%d" % i, "ident"], ["ps7"])
                    tt('dve', ya[:, h, t * 512:(t + 1) * 512], ps[7][:, :], gaT[:, t * 512:(t + 1) * 512], ALU.mult,
                       ["ps7", "gaT", "R2gen"], ["ya%d_%d" % (h, t)])

            if dbg == 'C':
                S.fence("dbgf")
                dump(0, ya[:, 0, :])
                dump(1, ya[:, 7, :])
                dump(2, ya[:, 3, :])
                raise _Stop()
            S.fence("R34gen")
            S.fence("R5gen")
            if do_final:
                dma('sp', nwb, fnwb_in[:, :], "sm", ["R34gen"], ["nwb"])
            ycat_keys = (["ya%d_%d" % (h, t) for h in range(NH) for t in range(NTT)]
                         + ["yb%d_%d" % (g, t) for g in range(4) for t in range(NTT)]
                         + ["yc%d_%d" % (c, t) for c in range(4) for t in range(NTT)])
            for tb in range(NTB):
                s = tb % 2
                t = tb // 4
                xk = ["xb%d_%d" % (s, c) for c in range(4)]
                dma('sp', xb[s], x_src[tb * 128:(tb + 1) * 128, :], "xb%d" % s,
                    ["xrow_%d_%d" % (li, tb), "R5gen"], xk)
                rk = (["ya%d_%d" % (h, t) for h in range(NH)] + ["yb%d_%d" % (g, t) for g in range(4)]
                      + ["yc%d_%d" % (c, t) for c in range(4)])
                for cg in range(4):
                    b = nbank()
                    for k in range(KC):
                        lhs = ya[:, k, tb * 128:(tb + 1) * 128] if k < 8 else R2hi[:, k - 8, tb * 128:(tb + 1) * 128]
                        mm(ps[b][:, :], lhs, R1[:, k, cg * 512:(cg + 1) * 512], k == 0, k == KC - 1,
                           rk + ["wo%d" % k], ["ps%d" % b])
                    tt('dve', xb[s][:, cg * 512:(cg + 1) * 512], ps[b][:, :], xb[s][:, cg * 512:(cg + 1) * 512], ALU.add,
                       ["ps%d" % b, xk[cg]], [xk[cg]])
                if do_final:
                    act(junk, xb[s], AF.Square, xk + ["R34gen"], ["junk", "ssD%d" % tb], accum_out=stA[:, tb:tb + 1])
                    act(stA[:, 16 + tb:17 + tb], stA[:, tb:tb + 1], AF.Sqrt, ["ssD%d" % tb], ["sdD%d" % tb],
                        scale=1.0 / D, bias=NORM_EPS)
                    recip(stA[:, 32 + tb:33 + tb], stA[:, 16 + tb:17 + tb], ["sdD%d" % tb], ["rsD%d" % tb])
                    stt(xb[s], xb[s], stA[:, 32 + tb:33 + tb], nwb, ALU.mult, ALU.mult,
                        xk + ["rsD%d" % tb, "nwb"], xk)
                dma('sp', x_dst[tb * 128:(tb + 1) * 128, :], xb[s], "xst%d" % s, xk,
                    ["xrow_%d_%d" % (li + 1, tb)])

    except _Stop:
        pass

    S.add('sp', lambda e: e.dma_start(out=fsc[:, 0:1], in_=flag_in[:, :]),
          ["xrow_%d_%d" % (L, tb) for tb in range(NTB)] + dbg_keys, ["_endscratch"], dma="end")

    S.resolve(lambda name: es.enter_context(nc.semaphore(name)))
    block = es.enter_context(nc.Block())

    @block.tensor
    def _(e):
        S.emit_engine('pe', e)

    @block.scalar
    def _(e):
        S.emit_engine('act', e)

    @block.vector
    def _(e):
        S.emit_engine('dve', e)

    @block.gpsimd
    def _(e):
        S.emit_engine('pool', e)

    @block.sync
    def _(e):
        S.emit_engine('sp', e)
        last = S.ops[-1]
        e.wait_ge(last['sigval'][0], last['sigval'][1])

    es.close()
    return nc


def _rep(v, n=128):
    return np.ascontiguousarray(np.broadcast_to(np.asarray(v, np.float32).reshape(1, -1), (n, v.size)))


def _host_inputs(layer_ids, x_core, p, w_in, w_out, norm_w, final_norm_w, lam_q1, lam_k1, lam_q2, lam_k2,
                 subln_w, sgu_ln_g, sgu_ln_b, w_s, b_s, conv_w, consts):
    Ls = list(layer_ids)
    m = dict(consts)
    m["x"] = np.ascontiguousarray(x_core, dtype=np.float32)
    m["w_in"] = np.ascontiguousarray(w_in[Ls])
    m["w_out"] = np.ascontiguousarray(w_out[Ls])
    m["nwb"] = np.stack([_rep(norm_w[l]) for l in Ls])
    m["fnwb"] = _rep(final_norm_w)
    m["lamv"] = np.stack([_rep(np.concatenate([lam_q1[l], lam_k1[l], lam_q2[l], lam_k2[l]])) for l in Ls])
    m["slnw"] = np.stack([_rep(subln_w[l]) for l in Ls])
    m["lng"] = np.stack([_rep(sgu_ln_g[l]) for l in Ls])
    m["lnb"] = np.stack([_rep(sgu_ln_b[l]) for l in Ls])
    m["wsT"] = np.stack([np.ascontiguousarray(np.transpose(w_s[l], (2, 0, 1)).reshape(128, 512)) for l in Ls])
    m["bsb"] = np.stack([_rep(b_s[l].reshape(-1)) for l in Ls])
    m["cw"] = np.stack([np.ascontiguousarray(np.transpose(conv_w[l].reshape(3, 4, 128), (2, 1, 0)).reshape(128, 12))
                        for l in Ls])
    m["flag"] = np.full((128, 1), float(p), np.float32)
    return m


def _consts(p):
    pos = (np.arange(T, dtype=np.float32) + np.float32(p * T)).astype(np.float32)
    inv_freq = (np.float32(10000.0) ** (-np.arange(0, 64, 2, dtype=np.float32) / np.float32(64))).astype(np.float32)
    ang = (pos[:, None] * inv_freq[None, :]).astype(np.float32)
    cosT = np.cos(ang).astype(np.float32).T
    sinT = np.sin(ang).astype(np.float32).T
    c = {}
    c["cosT"] = np.ascontiguousarray(np.tile(cosT, (4, 1)))
    c["sinT"] = np.ascontiguousarray(np.tile(sinT, (4, 1)))
    c["ident"] = np.eye(128, dtype=np.float32)
    R = np.zeros((128, 128), np.float32)
    for base in (0, 64):
        for i in range(32):
            R[base + i + 32, base + i] = -1.0
            R[base + i, base + i + 32] = 1.0
    c["rotR"] = R
    kk = np.arange(128)
    c["mask"] = (kk[None, :] >= kk[:, None]).astype(np.float32)
    return c


_NC_CACHE = {}


def _get_nc(layer_ids, final_norm):
    key = (tuple(layer_ids), final_norm)
    if key not in _NC_CACHE:
        _NC_CACHE[key] = build_program(list(layer_ids), final_norm)
    return _NC_CACHE[key]


def _run(layer_ids, final_norm, x_cores, params):
    nc = _get_nc(layer_ids, final_norm)
    in_maps = []
    for c in range(8):
        p = c % 2
        in_maps.append(_host_inputs(layer_ids, x_cores[c], p, consts=_consts(p), **params))
    res = run_bass_kernel_spmd(nc, in_maps, core_ids=list(range(8)))
    return [np.asarray(res.results[c]["out"], dtype=np.float32) for c in range(8)]


FUSED = True


def kernel(x, norm_w, w_in, lam_q1, lam_k1, lam_q2, lam_k2, subln_w, sgu_ln_g, sgu_ln_b,
           w_s, b_s, conv_w, w_out, final_norm_w):
    f = lambda a: np.asarray(a, dtype=np.float32)
    x = f(x)
    params = dict(w_in=f(w_in), w_out=f(w_out), norm_w=f(norm_w), final_norm_w=f(final_norm_w),
                  lam_q1=f(lam_q1), lam_k1=f(lam_k1), lam_q2=f(lam_q2), lam_k2=f(lam_k2),
                  subln_w=f(subln_w), sgu_ln_g=f(sgu_ln_g), sgu_ln_b=f(sgu_ln_b), w_s=f(w_s), b_s=f(b_s),
                  conv_w=f(conv_w))
    x_cores = [x[c // 2, (c % 2) * T:(c % 2 + 1) * T, :] for c in range(8)]
    if FUSED:
        outs = _run([0, 1], True, x_cores, params)
    else:
        mid = _run([0], False, x_cores, params)
        outs = _run([1], True, mid, params)
    out = np.empty((4, 4096, D), np.float32)
    for c in range(8):
        out[c // 2, (c % 2) * T:(c % 2 + 1) * T, :] = outs[c]
    return out
```

```python
import math
import os
from contextlib import ExitStack

import numpy as np
import concourse.bass as bass
import concourse.mybir as mybir
from concourse.bass_utils import run_bass_kernel_spmd

F32 = mybir.dt.float32
BF16 = mybir.dt.bfloat16
AF = mybir.ActivationFunctionType
ALU = mybir.AluOpType
AX = mybir.AxisListType

DEPTH = 2
D = 2048
T = 2048
NTB = 16
NTT = 4
KC = 16
PROJ = 7680
NH = 8
NORM_EPS = 1e-5
LN_EPS = 1e-5
GROUPS = [[0, 1], [2, 3], [4, 5], [6, 7]]
SEM_LIMIT = 6000
PSUM_EXCL = True


class _Stop(Exception):
    pass


class Sched:
    def __init__(self):
        self.ops = []

    def add(self, eng, fn, reads=(), writes=(), dma=None, inc=16):
        self.ops.append(dict(eng=eng, fn=fn, reads=tuple(reads), writes=tuple(writes),
                             dma=dma, inc=inc, deps=(), sig=False, sigval=None, waits=()))

    def fence(self, key):
        self.add('pool', self._fence_fn, reads=(), writes=(key, "_fence_scratch"))
        self.ops[-1]['barrier'] = True

    def resolve(self, new_sem):
        ops = self.ops
        W = {}
        R = {}
        last_eng = {}
        dma_since = []
        last_bar = None
        for i, op in enumerate(ops):
            deps = set()
            if op.get('barrier'):
                deps.update(last_eng.values())
                deps.update(dma_since)
            if last_bar is not None:
                deps.add(last_bar)
            for k in op['reads']:
                w = W.get(k)
                if w:
                    deps.update(w['eng'].values())
                    deps.update(w['dma'])
                if PSUM_EXCL and k.startswith("ps") and k[2:].isdigit():
                    r = R.get(k)
                    if r:
                        deps.update(j for e2, j in r['eng'].items() if e2 != op['eng'])
            for k in op['writes']:
                w = W.get(k)
                if w:
                    deps.update(w['eng'].values())
                    deps.update(w['dma'])
                r = R.get(k)
                if r:
                    deps.update(r['eng'].values())
                    deps.update(r['dma'])
            deps.discard(i)
            keep = []
            for j in deps:
                oj = ops[j]
                if oj['eng'] == 'pe' and op['eng'] == 'pe' and oj['dma'] is None and op['dma'] is None:
                    continue
                keep.append(j)
                oj['sig'] = True
            op['deps'] = keep
            if op.get('barrier'):
                last_bar = i
                dma_since = []
                op['sig'] = True
            if op['dma'] is not None:
                dma_since.append(i)
            else:
                last_eng[op['eng']] = i
            for k in op['reads']:
                r = R.setdefault(k, {'eng': {}, 'dma': []})
                if op['dma'] is not None:
                    r['dma'].append(i)
                else:
                    r['eng'][op['eng']] = i
            for k in op['writes']:
                if op['dma'] is not None:
                    w = W.setdefault(k, {'eng': {}, 'dma': []})
                    w['eng'] = {}
                    w['dma'].append(i)
                else:
                    W[k] = {'eng': {op['eng']: i}, 'dma': []}
                R[k] = {'eng': {}, 'dma': []}
        cnt = {}
        cur = {}
        dcnt = {}
        dsem = {}
        for i, op in enumerate(ops):
            waits = {}
            for j in op['deps']:
                oj = ops[j]
                if oj['dma'] is not None:
                    s, v = dsem[oj['dma']], dcnt[oj['dma']]
                else:
                    s, v = oj['sigval']
                if waits.get(s, (None, 0))[1] < v:
                    waits[s] = (s, v)
            op['waits'] = list(waits.values())
            if op['dma'] is not None:
                k = op['dma']
                if k not in dsem:
                    dsem[k] = new_sem("d_" + k)
                    dcnt[k] = 0
                dcnt[k] += op['inc']
                op['sigval'] = (dsem[k], dcnt[k])
            elif op['sig']:
                e = op['eng']
                if e not in cur or cnt[e] >= SEM_LIMIT:
                    cur[e] = new_sem("c_%s_%d" % (e, i))
                    cnt[e] = 0
                cnt[e] += 1
                op['sigval'] = (cur[e], cnt[e])

    def emit_engine(self, eng, eobj):
        known = {}
        n = 0
        for op in self.ops:
            if op['eng'] != eng:
                continue
            for (s, v) in op['waits']:
                kid = id(s)
                if known.get(kid, 0) >= v:
                    continue
                known[kid] = v
                eobj.wait_ge(s, v)
            ins = op['fn'](eobj)
            n += 1
            if op['dma'] is not None:
                ins.then_inc(op['sigval'][0], op['inc'])
            elif op['sig']:
                ins.then_inc(op['sigval'][0], 1)
        return n


def build_program(layer_ids, final_norm, dbg=None):
    L = len(layer_ids)
    nc = bass.Bass("TRN2", target_bir_lowering=False)
    S = Sched()
    es = ExitStack()

    def din(name, shape, dt=F32):
        return nc.dram_tensor(name, list(shape), dt, kind="ExternalInput")

    x_in = din("x", [T, D])
    w_in = din("w_in", [L, D, PROJ])
    w_out = din("w_out", [L, D, D])
    nwb_in = din("nwb", [L, 128, D])
    fnwb_in = din("fnwb", [128, D])
    cos_in = din("cosT", [128, T])
    sin_in = din("sinT", [128, T])
    lamv_in = din("lamv", [L, 128, 256])
    slnw_in = din("slnw", [L, 128, 128])
    lng_in = din("lng", [L, 128, 512])
    lnb_in = din("lnb", [L, 128, 512])
    wsT_in = din("wsT", [L, 128, 512])
    bsb_in = din("bsb", [L, 128, 512])
    cw_in = din("cw", [L, 128, 12])
    flag_in = din("flag", [128, 1])
    ident_in = din("ident", [128, 128])
    rotR_in = din("rotR", [128, 128])
    mask_in = din("mask", [128, 128])
    out_t = nc.dram_tensor("out", [T, D], F32, kind="ExternalOutput")

    xres = [nc.dram_tensor("xres%d" % i, [T, D], F32) for i in range(max(L - 1, 0))]
    kvin = [[nc.dram_tensor("kvin%d_%d" % (l, h), [256, 2048], BF16) for h in range(NH)] for l in range(L)]
    kvout = [[nc.dram_tensor("kvout%d_%d" % (l, h), [512, 2048], BF16) for h in range(NH)] for l in range(L)]
    q_s = [nc.dram_tensor("qs%d" % l, [1024, 2048], BF16) for l in range(L)]
    ga_s = [nc.dram_tensor("gas%d" % l, [1024, 2048], BF16) for l in range(L)]
    zt_i = [nc.dram_tensor("zti%d" % l, [128, 8], F32) for l in range(L)]
    zt_o = [nc.dram_tensor("zto%d" % l, [256, 8], F32) for l in range(L)]

    def sb(name, shape, dt=F32):
        return es.enter_context(nc.sbuf_tensor(name + "_sb", list(shape), dt))

    R1 = sb("R1", [128, 16, 2048], BF16)
    R2lo = sb("R2lo", [128, 4, 2048], F32)
    R2hi = sb("R2hi", [128, 8, 2048], BF16)
    R34 = sb("R34", [128, 16384], BF16)
    R5 = sb("R5", [128, 4096], F32)
    qbf = [sb("qbf%d" % i, [128, 512], BF16) for i in range(2)]
    vn = sb("vn", [128, 512], BF16)
    ident = sb("ident", [128, 128], F32)
    rotR = sb("rotRb", [128, 128], BF16)
    maskb = sb("maskb", [128, 128], BF16)
    maskf = sb("maskf", [128, 128], F32)
    flag = sb("flagt", [128, 1], F32)
    lamv = sb("lamvt", [128, 256], F32)
    lamt = sb("lamt", [128, 8], F32)
    slnw = sb("slnwt", [128, 128], F32)
    wsTb = sb("wsTb", [128, 4, 128], BF16)
    bsb = sb("bsbt", [128, 4, 128], F32)
    cw = sb("cwt", [128, 4, 3], F32)
    stA = sb("stA", [128, 48], F32)
    stB = sb("stB", [128, 8], F32)
    stC = sb("stC", [128, 32], F32)
    gg01 = sb("gg01", [128, 4, 2], F32)
    ztail = sb("ztail", [128, 4, 2], F32)
    hz = sb("hz", [128, 4, 2], F32)
    corr = sb("corr", [128, 4, 2], F32)
    fsc = sb("fsc", [128, 1], F32)
    junk2 = sb("junk2", [128, 512], BF16)

    ps = [es.enter_context(nc.psum_tensor("ps%d" % i, [128, 512], F32)) for i in range(8)]

    S._fence_fn = lambda e: e.memset(fsc[:, 0:1], 0.0)

    ya = R2lo[:, :, :].bitcast(BF16).rearrange("p g (two n) -> p (g two) n", two=2)
    stg = [ya[:, 0, :], ya[:, 1, :]]
    vst = [ya[:, 2, :].rearrange("p (b n) -> p b n", b=4), ya[:, 3, :].rearrange("p (b n) -> p b n", b=4)]
    wbuf = [R34[:, s * 8192:(s + 1) * 8192].rearrange("p (k n) -> p k n", k=16) for s in range(2)]
    nwb = R34[:, 8192:12288].bitcast(F32)
    junk = R34[:, 12288:14336]
    kTl = R34[:, 0:2048]
    kTr = R34[:, 2048:4096]
    qT = R34[:, 4096:6144]
    gaT = R34[:, 6144:8192]
    Vl = R34[:, 8192:8192 + 2064].rearrange("p (b n) -> p b n", b=16)
    Vrr = R34[:, 10256:10256 + 2064].rearrange("p (b n) -> p b n", b=16)
    Vr = R34[:, 12320:12320 + 2064].rearrange("p (b n) -> p b n", b=16)
    Pb = [R34[:, 14384 + i * 512:14384 + (i + 1) * 512] for i in range(3)]
    xb = [R5[:, 0:2048], R5[:, 2048:4096]]
    T5 = [R5[:, i * 512:(i + 1) * 512] for i in range(8)]
    TR = [R2lo[:, 2, i * 512:(i + 1) * 512] for i in range(4)]
    lng, lnb = T5[6], T5[7]
    wsTf = R34[:, 0:1024].bitcast(F32).rearrange("p (g n) -> p g n", g=4)
    cosT, sinT = xb[0], xb[1]

    def dma(eng, out, in_, key, reads, writes):
        S.add(eng, lambda e: e.dma_start(out=out, in_=in_), reads, writes, dma=key)

    def mm(out, lhsT, rhs, start, stop, reads, writes):
        S.add('pe', lambda e: e.matmul(out, lhsT, rhs, start=start, stop=stop), reads, writes)

    def tr(out, in_, reads, writes):
        S.add('pe', lambda e: e.transpose(out, in_, ident[:, :]), reads, writes)

    def act(out, in_, func, reads, writes, bias=None, scale=None, accum_out=None):
        kw = {}
        if bias is not None:
            kw['bias'] = bias
        if scale is not None:
            kw['scale'] = scale
        if accum_out is not None:
            kw['accum_out'] = accum_out
        S.add('act', lambda e: e.activation(out, in_, func, **kw), reads, writes)

    def tt(eng, out, in0, in1, op, reads, writes):
        S.add(eng, lambda e: e.tensor_tensor(out, in0, in1, op), reads, writes)

    def ts(eng, out, in0, s1, s2, op0, op1, reads, writes):
        if op1 is None:
            S.add(eng, lambda e: e.tensor_scalar(out, in0, s1, None, op0), reads, writes)
        else:
            S.add(eng, lambda e: e.tensor_scalar(out, in0, s1, s2, op0, op1), reads, writes)

    def stt(out, in0, scalar, in1, op0, op1, reads, writes):
        S.add('dve', lambda e: e.scalar_tensor_tensor(out, in0, scalar, in1, op0, op1), reads, writes)

    def recip(out, in_, reads, writes):
        S.add('dve', lambda e: e.reciprocal(out, in_), reads, writes)

    def cp(eng, out, in_, reads, writes):
        S.add(eng, lambda e: e.tensor_copy(out, in_), reads, writes)

    def memset(eng, ap, val, reads, writes):
        S.add(eng, lambda e: e.memset(ap, val), reads, writes)

    bank_ctr = [0]

    def nbank():
        b = bank_ctr[0] % 8
        bank_ctr[0] += 1
        return b

    dbg_t = nc.dram_tensor("dbg", [128, 8 * 2048], F32, kind="ExternalOutput") if dbg else None
    dbg_keys = []

    def dump(slot, src_ap, dram=False):
        s2 = slot % 2
        if dram:
            dma('sp', R34[:, 0:2048], src_ap, "dstage", [], ["dstage"])
            src_ap = R34[:, 0:2048]
        cp('dve', xb[s2], src_ap, ["dstage"], ["xbd%d" % s2])
        dma('sp', dbg_t[:, slot * 2048:(slot + 1) * 2048], xb[s2], "dbg%d" % s2, ["xbd%d" % s2], ["dbgout%d" % slot])
        dbg_keys.append("dbgout%d" % slot)

    dma('sp', ident[:, :], ident_in[:, :], "cst", [], ["ident"])
    dma('sp', maskf[:, :], mask_in[:, :], "cst", [], ["maskf"])
    dma('sp', flag[:, :], flag_in[:, :], "cst", [], ["flag"])
    dma('pool', rotR[:, :], rotR_in[:, :], "cstp", [], ["rotR"])
    dma('pool', maskb[:, :], mask_in[:, :], "cstp", [], ["maskb"])

    hT_keys = ["hT%d_%d" % (tb, q4) for tb in range(NTB) for q4 in range(4)]

    try:
        for li, layer in enumerate(layer_ids):
            lam_init = 0.8 - 0.6 * math.exp(-0.3 * layer)
            last = (li == L - 1)
            x_src = x_in if li == 0 else xres[li - 1]
            x_dst = out_t if last else xres[li]
            do_final = last and final_norm
            LK = "L%d" % li

            def xkey(t, tb):
                return "dram_%s_%d" % (t.name if hasattr(t, "name") else id(t), tb)

            S.fence("R34gen")
            dma('sp', nwb, nwb_in[li, :, :], "sm", ["R34gen"], ["nwb"])
            dma('sp', lamv[:, :], lamv_in[li, :, :], "sm", [], ["lamv"])
            dma('sp', slnw[:, :], slnw_in[li, :, :], "sm", [], ["slnw"])
            dma('sp', wsTf, wsT_in[li, :, :].rearrange("p (g n) -> p g n", g=4), "sm", [], ["wsTf"])
            dma('sp', bsb[:, :, :], bsb_in[li, :, :].rearrange("p (g n) -> p g n", g=4), "sm", [], ["bsb"])
            dma('sp', cw[:, :, :], cw_in[li, :, :].rearrange("p (g n) -> p g n", g=4), "sm", [], ["cw"])
            nlam = lamt[:, 5:6]
            _sk = os.environ.get('SKIPSM', '0')
            if True:
                if _sk not in ('1', '2'):
                    tt('dve', lamv[:, 0:64], lamv[:, 0:64], lamv[:, 64:128], ALU.mult, ["lamv"], ["lamp1"])
                    tt('dve', lamv[:, 128:192], lamv[:, 128:192], lamv[:, 192:256], ALU.mult, ["lamv"], ["lamp2"])
                    if _sk != '4':
                        act(junk2[:, 0:64], lamv[:, 0:64], AF.Copy, ["lamp1"], ["junk2", "lams1"], accum_out=lamt[:, 0:1])
                        act(junk2[:, 0:64], lamv[:, 128:192], AF.Copy, ["lamp2"], ["junk2", "lams2"], accum_out=lamt[:, 1:2])
                        act(lamt[:, 2:3], lamt[:, 0:1], AF.Exp, ["lams1"], ["lame1"])
                        act(lamt[:, 3:4], lamt[:, 1:2], AF.Exp, ["lams2"], ["lame2"])
                    if _sk not in ('4', '5'):
                        tt('dve', lamt[:, 4:5], lamt[:, 3:4], lamt[:, 2:3], ALU.subtract, ["lame1", "lame2"], ["lamd"])
                        ts('dve', lamt[:, 5:6], lamt[:, 4:5], -lam_init, None, ALU.add, None, ["lamd"], ["nlam"])
                nlam = lamt[:, 5:6]
                if _sk not in ('1', '3'):
                    for g in range(4):
                        tt('dve', wsTb[:, g, :], wsTf[:, g, :], maskf[:, :], ALU.mult, ["wsTf", "maskf"], ["wsTb%d" % g])
                    ts('dve', slnw[:, :], slnw[:, :], 1.0 - lam_init, None, ALU.mult, None, ["slnw"], ["slnw"])


            S.fence("R1gen")
            S.fence("R5gen")
            for tb in range(0 if os.environ.get('SKIPA') else NTB):
                s = tb % 2
                xk = ["xb%d_%d" % (s, c) for c in range(4)]
                dma('sp', xb[s], x_src[tb * 128:(tb + 1) * 128, :], "xb%d" % s,
                    ["xrow_%d_%d" % (li, tb), "R5gen"], xk)
                act(junk, xb[s], AF.Square, xk + ["R34gen"], ["junk", "ssA%d" % tb], accum_out=stA[:, tb:tb + 1])
                act(stA[:, 16 + tb:17 + tb], stA[:, tb:tb + 1], AF.Sqrt, ["ssA%d" % tb], ["sdA%d" % tb],
                    scale=1.0 / D, bias=NORM_EPS)
                recip(stA[:, 32 + tb:33 + tb], stA[:, 16 + tb:17 + tb], ["sdA%d" % tb], ["rsA%d" % tb])
                stt(xb[s], xb[s], stA[:, 32 + tb:33 + tb], nwb, ALU.mult, ALU.mult,
                    xk + ["rsA%d" % tb, "nwb"], xk)
                for q4 in range(4):
                    b = nbank()
                    for j in range(4):
                        c = q4 * 4 + j
                        tr(ps[b][:, j * 128:(j + 1) * 128], xb[s][:, c * 128:(c + 1) * 128],
                           xk + ["ident"], ["ps%d" % b])
                    eng = 'dve' if q4 % 2 == 0 else 'act'
                    outv = R1[:, q4 * 4:(q4 + 1) * 4, tb * 128:(tb + 1) * 128]
                    inv = ps[b][:, :].rearrange("p (j n) -> p j n", j=4)
                    if eng == 'dve':
                        cp('dve', outv, inv, ["ps%d" % b, "R1gen"], ["hT%d_%d" % (tb, q4)])
                    else:
                        act(outv, inv, AF.Copy, ["ps%d" % b, "R1gen"], ["hT%d_%d" % (tb, q4)])

            if dbg == 'A':
                S.fence("dbgf")
                dump(0, R1[:, 0, :])
                dump(1, R1[:, 15, :])
                raise _Stop()
            S.fence("R34gen")
            S.fence("R5gen")
            S.fence("R2gen")
            S.fence("TMPgen")
            dma('sp', cosT, cos_in[:, :], "cs", ["R5gen"], ["cosT"])
            dma('sp', sinT, sin_in[:, :], "cs", ["R5gen"], ["sinT"])
            wslot = [0]

            def load_w(blocks):
                s = wslot[0] % 2
                wslot[0] += 1
                off = 0
                for (c0, n) in blocks:
                    for kh in range(2):
                        dma('pool', wbuf[s][:, kh * 8:(kh + 1) * 8, off:off + n],
                            w_in[li, kh * 1024:(kh + 1) * 1024, c0:c0 + n].rearrange("(k p) n -> p k n", p=128),
                            "w%d" % s, ["R34gen"], ["w%d" % s])
                    off += n
                return s

            def fm_block(s, cb, t):
                b = nbank()
                rk = ["w%d" % s] + ["hT%d_%d" % (tb, q) for tb in range(4 * t, 4 * t + 4) for q in range(4)]
                for k in range(KC):
                    mm(ps[b][:, :], wbuf[s][:, k, cb * 128:(cb + 1) * 128], R1[:, k, t * 512:(t + 1) * 512],
                       k == 0, k == KC - 1, rk, ["ps%d" % b])
                return b

            def tm_block(s, tb):
                b = nbank()
                rk = ["w%d" % s] + ["hT%d_%d" % (tb, q) for q in range(4)]
                for k in range(KC):
                    mm(ps[b][:, :], R1[:, k, tb * 128:(tb + 1) * 128], wbuf[s][:, k, 0:512],
                       k == 0, k == KC - 1, rk, ["ps%d" % b])
                return b

            stg_ctr = [0]
            dbg_it = [0]

            def rope_job(col0, dst_fn, dkey_fn):
                s = load_w([(col0, 512)])

                def epilogue(b, cb, t, sg):
                    qi = t % 2
                    act(qbf[qi][:, :], ps[b][:, :], AF.Copy, ["ps%d" % b], ["qbf%d" % qi])
                    b2 = nbank()
                    mm(ps[b2][:, :], rotR[:, :], qbf[qi][:, :], True, True, ["rotR", "qbf%d" % qi], ["ps%d" % b2])
                    t1 = TR[2 * qi]
                    t2 = TR[2 * qi + 1]
                    tt('dve', t1, ps[b][:, :], cosT[:, t * 512:(t + 1) * 512], ALU.mult,
                       ["ps%d" % b, "cosT", "TMPgen"], ["T%d" % (2 * qi)])
                    tt('dve', t2, ps[b2][:, :], sinT[:, t * 512:(t + 1) * 512], ALU.mult,
                       ["ps%d" % b2, "sinT", "TMPgen"], ["T%d" % (2 * qi + 1)])
                    tt('pool', stg[sg][:, t * 512:(t + 1) * 512], t1, t2, ALU.add,
                       ["T%d" % (2 * qi), "T%d" % (2 * qi + 1), "R2gen"], ["stg%d_%d" % (sg, t)])
                    if t == NTT - 1:
                        dma('sp', dst_fn(cb), stg[sg], "stg%d" % sg, ["stg%d_%d" % (sg, tq) for tq in range(4)],
                            [dkey_fn(cb)])

                pending = None
                for cb in range(4):
                    sg = stg_ctr[0] % 2
                    stg_ctr[0] += 1
                    for t in range(NTT):
                        b = fm_block(s, cb, t)
                        if pending is not None:
                            epilogue(*pending)
                        pending = (b, cb, t, sg)
                epilogue(*pending)

            for hh in range(2):
                rope_job(1024 + hh * 512,
                         lambda cb, hh=hh: kvin[li][hh * 4 + cb][0:128, :],
                         lambda cb, hh=hh: "kvin%d_%d" % (li, hh * 4 + cb))
            if dbg == 'B1':
                S.fence("dbgf")
                dump(1, kvin[li][0][0:128, :], dram=True)
                raise _Stop()
            vst_ctr = [0]
            for hh in range(2):
                s = load_w([(2048 + hh * 512, 512)])
                for tq in range(4):
                    vs_ = vst_ctr[0] % 2
                    vst_ctr[0] += 1
                    for j in range(4):
                        tb = tq * 4 + j
                        b = tm_block(s, tb)
                        eng = 'dve' if j % 2 == 0 else 'act'
                        if eng == 'dve':
                            cp('dve', vst[vs_][:, j, :], ps[b][:, :], ["ps%d" % b, "R2gen"], ["vst%d_%d" % (vs_, j)])
                        else:
                            act(vst[vs_][:, j, :], ps[b][:, :], AF.Copy, ["ps%d" % b, "R2gen"], ["vst%d_%d" % (vs_, j)])
                    for cb in range(4):
                        h = hh * 4 + cb
                        dstv = kvin[li][h][128:256, :].rearrange("r (s e) -> (r s) e", e=128)
                        dstv = dstv[tq * 512:(tq + 1) * 512, :].rearrange("(b p) e -> p b e", p=128)
                        dma('sp', dstv, vst[vs_][:, :, cb * 128:(cb + 1) * 128], "vst%d" % vs_,
                            ["vst%d_%d" % (vs_, j) for j in range(4)], ["kvin%d_%d" % (li, h)])
            for h in range(NH):
                S.add('pool',
                      lambda e, h=h, li=li: e.collective_compute("AllGather", ALU.bypass, replica_groups=GROUPS,
                                                                 ins=[kvin[li][h].ap().opt()],
                                                                 outs=[kvout[li][h].ap().opt()]),
                      ["kvin%d_%d" % (li, h)], ["kvout%d_%d" % (li, h)], dma="cc", inc=1)
            if dbg == 'B2':
                S.fence("dbgf")
                dump(1, kvin[li][0][0:128, :], dram=True)
                dump(2, kvout[li][0][0:128, :], dram=True)
                dump(3, kvin[li][0][128:256, :], dram=True)
                raise _Stop()
            for hh in range(2):
                rope_job(hh * 512,
                         lambda cb, hh=hh: q_s[li][(hh * 4 + cb) * 128:(hh * 4 + cb + 1) * 128, :],
                         lambda cb, hh=hh: "qs%d_%d" % (li, hh * 4 + cb))
            for hh in range(2):
                s = load_w([(3072 + hh * 512, 512)])
                for cb in range(4):
                    h = hh * 4 + cb
                    sg = stg_ctr[0] % 2
                    stg_ctr[0] += 1
                    for t in range(NTT):
                        b = fm_block(s, cb, t)
                        act(stg[sg][:, t * 512:(t + 1) * 512], ps[b][:, :], AF.Silu, ["ps%d" % b, "R2gen"],
                            ["stg%d_%d" % (sg, t)])
                    dma('sp', ga_s[li][h * 128:(h + 1) * 128, :], stg[sg], "stg%d" % sg,
                        ["stg%d_%d" % (sg, t) for t in range(4)], ["gas%d_%d" % (li, h)])
            if dbg == 'B3':
                S.fence("dbgf")
                dump(0, q_s[li][0:128, :], dram=True)
                dump(1, kvin[li][0][0:128, :], dram=True)
                dump(2, kvout[li][0][0:128, :], dram=True)
                dump(3, kvin[li][0][128:256, :], dram=True)
                dump(4, ga_s[li][0:128, :], dram=True)
                raise _Stop()
            S.fence("R2gen")
            S.fence("TMPgen")
            mixb = R2lo
            dma('sp', lng, lng_in[li, :, :], "sm2", [], ["lng"])
            dma('sp', lnb, lnb_in[li, :, :], "sm2", [], ["lnb"])
            s = load_w([(4608, 512)])
            for tb in range(NTB):
                b = tm_block(s, tb)
                pk = ["ps%d" % b]
                act(junk2[:, :], ps[b][:, :], AF.Copy, pk, ["junk2", "lnsm"], accum_out=stB[:, 0:1])
                act(junk2[:, :], ps[b][:, :], AF.Square, pk, ["junk2", "lnsq"], accum_out=stB[:, 1:2])
                ts('dve', stB[:, 2:3], stB[:, 0:1], 1.0 / 512, None, ALU.mult, None, ["lnsm"], ["lnmean"])
                tt('dve', stB[:, 3:4], stB[:, 2:3], stB[:, 2:3], ALU.mult, ["lnmean"], ["lnmsq"])
                stt(stB[:, 4:5], stB[:, 1:2], 1.0 / 512, stB[:, 3:4], ALU.mult, ALU.subtract,
                    ["lnsq", "lnmsq"], ["lnvar"])
                act(stB[:, 5:6], stB[:, 4:5], AF.Sqrt, ["lnvar"], ["lnsd"], bias=LN_EPS)
                recip(stB[:, 6:7], stB[:, 5:6], ["lnsd"], ["lnrs"])
                ts('dve', T5[0], ps[b][:, :], stB[:, 2:3], stB[:, 6:7], ALU.subtract, ALU.mult,
                   pk + ["lnmean", "lnrs", "TMPgen"], ["T0"])
                tt('pool', T5[1], T5[0], lng, ALU.mult, ["T0", "lng", "TMPgen"], ["T1"])
                tt('pool', vn[:, :], T5[1], lnb, ALU.add, ["T1", "lnb"], ["vn"])
                b2 = nbank()
                for g in range(4):
                    mm(ps[b2][:, g * 128:(g + 1) * 128], vn[:, g * 128:(g + 1) * 128], wsTb[:, g, :], True, True,
                       ["vn", "wsTb%d" % g], ["ps%d" % b2])
                tt('dve', mixb[:, :, tb * 128:(tb + 1) * 128], ps[b2][:, :].rearrange("p (g n) -> p g n", g=4),
                   bsb[:, :, :], ALU.add, ["ps%d" % b2, "bsb", "R2gen"], ["mixb%d" % tb])
            for gp in range(2):
                s = load_w([(4096 + (2 * gp) * 128, 128), (5120 + (2 * gp) * 128, 128),
                            (4096 + (2 * gp + 1) * 128, 128), (5120 + (2 * gp + 1) * 128, 128)])
                for gi in range(2):
                    g = 2 * gp + gi
                    for t in range(NTT):
                        bu = fm_block(s, 2 * gi, t)
                        bg = fm_block(s, 2 * gi + 1, t)
                        act(T5[2], ps[bg][:, :], AF.Silu, ["ps%d" % bg, "TMPgen"], ["T2"])
                        tt('dve', T5[3], ps[bu][:, :], mixb[:, g, t * 512:(t + 1) * 512], ALU.mult,
                           ["ps%d" % bu, "TMPgen"] + ["mixb%d" % tb for tb in range(4 * t, 4 * t + 4)], ["T3"])
                        tt('pool', R2hi[:, g, t * 512:(t + 1) * 512], T5[2], T5[3], ALU.mult,
                           ["T2", "T3"], ["yb%d_%d" % (g, t)])
            if dbg == 'B4':
                S.fence("dbgf")
                dump(0, q_s[li][0:128, :], dram=True)
                dump(1, kvin[li][0][0:128, :], dram=True)
                dump(2, kvout[li][0][0:128, :], dram=True)
                dump(3, kvin[li][0][128:256, :], dram=True)
                dump(4, ga_s[li][0:128, :], dram=True)
                dump(5, R2hi[:, 0, :])
                raise _Stop()
            S.fence("TMPgen")
            zbuf = R5[:, 2048:3072]
            for c in range(4):
                s = load_w([(5632 + c * 128, 128), (6656 + c * 128, 128), (6144 + c * 128, 128), (7168 + c * 128, 128)])
                memset('pool', zbuf[:, 0:2], 0.0, ["TMPgen", "zcarry"], ["zcarry"])
                for t in range(NTT):
                    bx = fm_block(s, 0, t)
                    bc = fm_block(s, 1, t)
                    bb = fm_block(s, 2, t)
                    bgc = fm_block(s, 3, t)
                    act(T5[0], ps[bx][:, :], AF.Copy, ["ps%d" % bx, "TMPgen"], ["T0"])
                    tt('dve', zbuf[:, 2:514], ps[bc][:, :], T5[0], ALU.mult, ["ps%d" % bc, "T0", "TMPgen"], ["zmain"])
                    ts('dve', T5[1], zbuf[:, 2:514], cw[:, c, 2:3], None, ALU.mult, None,
                       ["zmain", "cw", "TMPgen"], ["T1"])
                    stt(T5[1], zbuf[:, 1:513], cw[:, c, 1:2], T5[1], ALU.mult, ALU.add,
                        ["zmain", "zcarry", "cw", "T1"], ["T1"])
                    stt(T5[1], zbuf[:, 0:512], cw[:, c, 0:1], T5[1], ALU.mult, ALU.add,
                        ["zmain", "zcarry", "cw", "T1"], ["T1"])
                    act(T5[2], ps[bgc][:, :], AF.Silu, ["ps%d" % bgc, "TMPgen"], ["T2"])
                    tt('dve', T5[3], ps[bb][:, :], T5[2], ALU.mult, ["ps%d" % bb, "T2", "TMPgen"], ["T3"])
                    tt('pool', R2hi[:, 4 + c, t * 512:(t + 1) * 512], T5[1], T5[3], ALU.mult,
                       ["T1", "T3"], ["yc%d_%d" % (c, t)])
                    if t == 0:
                        cp('pool', gg01[:, c, :], T5[3][:, 0:2], ["T3"], ["gg01_%d" % c])
                    if t == NTT - 1:
                        cp('pool', ztail[:, c, :], zbuf[:, 512:514], ["zmain"], ["ztail%d" % c])
                    else:
                        cp('pool', zbuf[:, 0:2], zbuf[:, 512:514], ["zmain", "zcarry"], ["zcarry"])
            dma('sp', zt_i[li][:, :], ztail[:, :, :].rearrange("p a b -> p (a b)"), "zts",
                ["ztail%d" % c for c in range(4)], ["zti%d" % li])
            S.add('pool',
                  lambda e, li=li: e.collective_compute("AllGather", ALU.bypass, replica_groups=GROUPS,
                                                        ins=[zt_i[li].ap().opt()], outs=[zt_o[li].ap().opt()]),
                  ["zti%d" % li], ["zto%d" % li], dma="cc", inc=1)
            dma('sp', hz[:, :, :].rearrange("p a b -> p (a b)"), zt_o[li][0:128, :], "ztl", ["zto%d" % li], ["hz"])
            ts('dve', hz[:, :, :], hz[:, :, :], flag[:, 0:1], None, ALU.mult, None, ["hz", "flag"], ["hz"])
            for c in range(4):
                tt('dve', corr[:, c, 0:1], hz[:, c, 1:2], cw[:, c, 1:2], ALU.mult, ["hz", "cw"], ["corr%d" % c])
                stt(corr[:, c, 0:1], hz[:, c, 0:1], cw[:, c, 0:1], corr[:, c, 0:1], ALU.mult, ALU.add,
                    ["hz", "cw", "corr%d" % c], ["corr%d" % c])
                tt('dve', corr[:, c, 1:2], hz[:, c, 1:2], cw[:, c, 0:1], ALU.mult, ["hz", "cw", "corr%d" % c], ["corr%d" % c])
                tt('dve', corr[:, c, :], corr[:, c, :], gg01[:, c, :], ALU.mult, ["corr%d" % c, "gg01_%d" % c], ["corr%d" % c])
                tt('dve', R2hi[:, 4 + c, 0:2], R2hi[:, 4 + c, 0:2], corr[:, c, :], ALU.add,
                   ["corr%d" % c, "yc%d_0" % c], ["yc%d_0" % c])

            if dbg == 'B':
                S.fence("dbgf")
                dump(0, q_s[li][0:128, :], dram=True)
                dump(1, kvin[li][0][0:128, :], dram=True)
                dump(2, kvout[li][0][0:128, :], dram=True)
                dump(3, kvin[li][0][128:256, :], dram=True)
                dump(4, ga_s[li][0:128, :], dram=True)
                dump(5, R2hi[:, 0, :])
                dump(6, R2hi[:, 4, :])
                dump(7, R2hi[:, 7, :])
                raise _Stop()
            S.fence("R34gen")
            S.fence("R1gen")
            S.fence("R2gen")
            S.fence("TMPgen")
            for k in range(KC):
                dma('pool', R1[:, k, :], w_out[li, k * 128:(k + 1) * 128, :], "wo", ["R1gen"], ["wo%d" % k])
            memset('pool', Vl[:, :, 128:129], 1.0, ["R34gen"], ["Vl1"])
            memset('pool', Vrr[:, :, 128:129], 1.0, ["R34gen"], ["Vrr1"])
            O1 = T5[0].rearrange("p (i n) -> p i n", i=4)
            Ob = T5[1].rearrange("p (i n) -> p i n", i=4)
            yn = T5[2].rearrange("p (i n) -> p i n", i=4)
            sctr = [0]
            pctr = [0]
            for h in range(NH):
                dma('sp', kTl, kvin[li][h][0:128, :], "kTl", ["kvin%d_%d" % (li, h), "R34gen"], ["kTl"])
                dma('sp', kTr, kvout[li][h][0:128, :], "kTr", ["kvout%d_%d" % (li, h), "R34gen"], ["kTr"])
                dma('sp', qT, q_s[li][h * 128:(h + 1) * 128, :], "qT", ["qs%d_%d" % (li, h), "R34gen"], ["qT"])
                dma('sp', gaT, ga_s[li][h * 128:(h + 1) * 128, :], "gaT", ["gas%d_%d" % (li, h), "R34gen"], ["gaT"])
                srcl = kvin[li][h][128:256, :].rearrange("r (s e) -> (r s) e", e=128).rearrange("(b p) e -> p b e", p=128)
                srcr = kvout[li][h][128:256, :].rearrange("r (s e) -> (r s) e", e=128).rearrange("(b p) e -> p b e", p=128)
                dma('sp', Vl[:, :, 0:128], srcl, "Vl", ["kvin%d_%d" % (li, h), "R34gen"], ["Vl"])
                dma('sp', Vrr[:, :, 0:128], srcr, "Vrr", ["kvout%d_%d" % (li, h), "R34gen"], ["Vrr"])
                ts('pool', Vr[:, :, :], Vrr[:, :, :], flag[:, 0:1], None, ALU.mult, None,
                   ["Vrr", "Vrr1", "flag", "R34gen"], ["Vr"])
                for t in range(NTT):
                    for m in range(2):
                        visits = [('r', j) for j in range(16)] + [('l', j) for j in range(4 * t + 4)]
                        LOOK = 2

                        def emit_qk(vi, src, j):
                            diag = (src == 'l' and j >= 4 * t)
                            r = (j - 4 * t) if diag else 0
                            N = 512 - 128 * r
                            sbk = 4 + (sctr[0] % 3)
                            sctr[0] += 1
                            pbi = pctr[0] % 3
                            pctr[0] += 1
                            kT = kTl if src == 'l' else kTr
                            kk = "kTl" if src == 'l' else "kTr"
                            mm(ps[sbk][:, 0:N], kT[m * 64:(m + 1) * 64, j * 128:(j + 1) * 128],
                               qT[m * 64:(m + 1) * 64, t * 512 + r * 128:(t + 1) * 512], True, True,
                               [kk, "qT"], ["ps%d" % sbk])
                            act(Pb[pbi][:, 0:N], ps[sbk][:, 0:N], AF.Exp, ["ps%d" % sbk, "R34gen"], ["P%d" % pbi],
                                scale=0.125)
                            if diag:
                                tt('pool', Pb[pbi][:, 0:128], Pb[pbi][:, 0:128], maskb[:, :], ALU.mult,
                                   ["P%d" % pbi, "maskb"], ["P%d" % pbi])
                            return (r, pbi)

                        def emit_pv(vi, src, j, r, pbi):
                            Vt = Vl if src == 'l' else Vr
                            vk = ["Vl", "Vl1"] if src == 'l' else ["Vr"]
                            for i in range(r, 4):
                                mm(ps[i][:, 0:129], Pb[pbi][:, (i - r) * 128:(i - r + 1) * 128], Vt[:, j, 0:129],
                                   vi == 0, (src == 'l' and j == 4 * t + i), ["P%d" % pbi] + vk, ["ps%d" % i])

                        pend = {}
                        nv = len(visits)
                        for vi in range(min(LOOK, nv)):
                            pend[vi] = emit_qk(vi, *visits[vi])
                        for vi in range(nv):
                            if vi + LOOK < nv:
                                pend[vi + LOOK] = emit_qk(vi + LOOK, *visits[vi + LOOK])
                            r_, pbi_ = pend.pop(vi)
                            emit_pv(vi, visits[vi][0], visits[vi][1], r_, pbi_)
                        for i in range(4):
                            if m == 0:
                                recip(stC[:, i:i + 1], ps[i][:, 128:129], ["ps%d" % i], ["rl1_%d" % i])
                                ts('dve', O1[:, i, :], ps[i][:, 0:128], stC[:, i:i + 1], None, ALU.mult, None,
                                   ["ps%d" % i, "rl1_%d" % i, "TMPgen"], ["O1_%d" % i])
                            else:
                                recip(stC[:, 4 + i:5 + i], ps[i][:, 128:129], ["ps%d" % i], ["rl2_%d" % i])
                                tt('dve', stC[:, 8 + i:9 + i], stC[:, 4 + i:5 + i], nlam, ALU.mult,
                                   ["rl2_%d" % i, "nlam"], ["nlr_%d" % i])
                                stt(Ob[:, i, :], ps[i][:, 0:128], stC[:, 8 + i:9 + i], O1[:, i, :], ALU.mult, ALU.add,
                                    ["ps%d" % i, "nlr_%d" % i, "O1_%d" % i, "TMPgen"], ["O_%d" % i])
                    for i in range(4):
                        act(junk2[:, 0:128], Ob[:, i, :], AF.Square, ["O_%d" % i], ["junk2", "ssO%d" % i],
                            accum_out=stC[:, 12 + i:13 + i])
                        act(stC[:, 16 + i:17 + i], stC[:, 12 + i:13 + i], AF.Sqrt, ["ssO%d" % i], ["sdO%d" % i],
                            scale=1.0 / 128, bias=NORM_EPS)
                        recip(stC[:, 20 + i:21 + i], stC[:, 16 + i:17 + i], ["sdO%d" % i], ["rsO%d" % i])
                        stt(yn[:, i, :], Ob[:, i, :], stC[:, 20 + i:21 + i], slnw[:, :], ALU.mult, ALU.mult,
                            ["O_%d" % i, "rsO%d" % i, "slnw", "TMPgen"], ["yn%d" % i])
                        tr(ps[7][:, i * 128:(i + 1) * 128], yn[:, i, :], ["yn%d" % i, "ident"], ["ps7"])
                    tt('dve', ya[:, h, t * 512:(t + 1) * 512], ps[7][:, :], gaT[:, t * 512:(t + 1) * 512], ALU.mult,
                       ["ps7", "gaT", "R2gen"], ["ya%d_%d" % (h, t)])

            if dbg == 'C':
                S.fence("dbgf")
                dump(0, ya[:, 0, :])
                dump(1, ya[:, 7, :])
                dump(2, ya[:, 3, :])
                raise _Stop()
            S.fence("R34gen")
            S.fence("R5gen")
            if do_final:
                dma('sp', nwb, fnwb_in[:, :], "sm", ["R34gen"], ["nwb"])
            ycat_keys = (["ya%d_%d" % (h, t) for h in range(NH) for t in range(NTT)]
                         + ["yb%d_%d" % (g, t) for g in range(4) for t in range(NTT)]
                         + ["yc%d_%d" % (c, t) for c in range(4) for t in range(NTT)])
            for tb in range(NTB):
                s = tb % 2
                t = tb // 4
                xk = ["xb%d_%d" % (s, c) for c in range(4)]
                dma('sp', xb[s], x_src[tb * 128:(tb + 1) * 128, :], "xb%d" % s,
                    ["xrow_%d_%d" % (li, tb), "R5gen"], xk)
                rk = (["ya%d_%d" % (h, t) for h in range(NH)] + ["yb%d_%d" % (g, t) for g in range(4)]
                      + ["yc%d_%d" % (c, t) for c in range(4)])
                for cg in range(4):
                    b = nbank()
                    for k in range(KC):
                        lhs = ya[:, k, tb * 128:(tb + 1) * 128] if k < 8 else R2hi[:, k - 8, tb * 128:(tb + 1) * 128]
                        mm(ps[b][:, :], lhs, R1[:, k, cg * 512:(cg + 1) * 512], k == 0, k == KC - 1,
                           rk + ["wo%d" % k], ["ps%d" % b])
                    tt('dve', xb[s][:, cg * 512:(cg + 1) * 512], ps[b][:, :], xb[s][:, cg * 512:(cg + 1) * 512], ALU.add,
                       ["ps%d" % b, xk[cg]], [xk[cg]])
                if do_final:
                    act(junk, xb[s], AF.Square, xk + ["R34gen"], ["junk", "ssD%d" % tb], accum_out=stA[:, tb:tb + 1])
                    act(stA[:, 16 + tb:17 + tb], stA[:, tb:tb + 1], AF.Sqrt, ["ssD%d" % tb], ["sdD%d" % tb],
                        scale=1.0 / D, bias=NORM_EPS)
                    recip(stA[:, 32 + tb:33 + tb], stA[:, 16 + tb:17 + tb], ["sdD%d" % tb], ["rsD%d" % tb])
                    stt(xb[s], xb[s], stA[:, 32 + tb:33 + tb], nwb, ALU.mult, ALU.mult,
                        xk + ["rsD%d" % tb, "nwb"], xk)
                dma('sp', x_dst[tb * 128:(tb + 1) * 128, :], xb[s], "xst%d" % s, xk,
                    ["xrow_%d_%d" % (li + 1, tb)])

    except _Stop:
        pass

    S.add('sp', lambda e: e.dma_start(out=fsc[:, 0:1], in_=flag_in[:, :]),
          ["xrow_%d_%d" % (L, tb) for tb in range(NTB)] + dbg_keys, ["_endscratch"], dma="end")

    S.resolve(lambda name: es.enter_context(nc.semaphore(name)))
    block = es.enter_context(nc.Block())

    @block.tensor
    def _(e):
        S.emit_engine('pe', e)

    @block.scalar
    def _(e):
        S.emit_engine('act', e)

    @block.vector
    def _(e):
        S.emit_engine('dve', e)

    @block.gpsimd
    def _(e):
        S.emit_engine('pool', e)

    @block.sync
    def _(e):
        S.emit_engine('sp', e)
        last = S.ops[-1]
        e.wait_ge(last['sigval'][0], last['sigval'][1])

    es.close()
    return nc


def _rep(v, n=128):
    return np.ascontiguousarray(np.broadcast_to(np.asarray(v, np.float32).reshape(1, -1), (n, v.size)))


def _host_inputs(layer_ids, x_core, p, w_in, w_out, norm_w, final_norm_w, lam_q1, lam_k1, lam_q2, lam_k2,
                 subln_w, sgu_ln_g, sgu_ln_b, w_s, b_s, conv_w, consts):
    Ls = list(layer_ids)
    m = dict(consts)
    m["x"] = np.ascontiguousarray(x_core, dtype=np.float32)
    m["w_in"] = np.ascontiguousarray(w_in[Ls])
    m["w_out"] = np.ascontiguousarray(w_out[Ls])
    m["nwb"] = np.stack([_rep(norm_w[l]) for l in Ls])
    m["fnwb"] = _rep(final_norm_w)
    m["lamv"] = np.stack([_rep(np.concatenate([lam_q1[l], lam_k1[l], lam_q2[l], lam_k2[l]])) for l in Ls])
    m["slnw"] = np.stack([_rep(subln_w[l]) for l in Ls])
    m["lng"] = np.stack([_rep(sgu_ln_g[l]) for l in Ls])
    m["lnb"] = np.stack([_rep(sgu_ln_b[l]) for l in Ls])
    m["wsT"] = np.stack([np.ascontiguousarray(np.transpose(w_s[l], (2, 0, 1)).reshape(128, 512)) for l in Ls])
    m["bsb"] = np.stack([_rep(b_s[l].reshape(-1)) for l in Ls])
    m["cw"] = np.stack([np.ascontiguousarray(np.transpose(conv_w[l].reshape(3, 4, 128), (2, 1, 0)).reshape(128, 12))
                        for l in Ls])
    m["flag"] = np.full((128, 1), float(p), np.float32)
    return m


def _consts(p):
    pos = (np.arange(T, dtype=np.float32) + np.float32(p * T)).astype(np.float32)
    inv_freq = (np.float32(10000.0) ** (-np.arange(0, 64, 2, dtype=np.float32) / np.float32(64))).astype(np.float32)
    ang = (pos[:, None] * inv_freq[None, :]).astype(np.float32)
    cosT = np.cos(ang).astype(np.float32).T
    sinT = np.sin(ang).astype(np.float32).T
    c = {}
    c["cosT"] = np.ascontiguousarray(np.tile(cosT, (4, 1)))
    c["sinT"] = np.ascontiguousarray(np.tile(sinT, (4, 1)))
    c["ident"] = np.eye(128, dtype=np.float32)
    R = np.zeros((128, 128), np.float32)
    for base in (0, 64):
        for i in range(32):
            R[base + i + 32, base + i] = -1.0
            R[base + i, base + i + 32] = 1.0
    c["rotR"] = R
    kk = np.arange(128)
    c["mask"] = (kk[None, :] >= kk[:, None]).astype(np.float32)
    return c


_NC_CACHE = {}


def _get_nc(layer_ids, final_norm):
    key = (tuple(layer_ids), final_norm)
    if key not in _NC_CACHE:
        _NC_CACHE[key] = build_program(list(layer_ids), final_norm)
    return _NC_CACHE[key]


def _run(layer_ids, final_norm, x_cores, params):
    nc = _get_nc(layer_ids, final_norm)
    in_maps = []
    for c in range(8):
        p = c % 2
        in_maps.append(_host_inputs(layer_ids, x_cores[c], p, consts=_consts(p), **params))
    res = run_bass_kernel_spmd(nc, in_maps, core_ids=list(range(8)))
    return [np.asarray(res.results[c]["out"], dtype=np.float32) for c in range(8)]


FUSED = True


def kernel(x, norm_w, w_in, lam_q1, lam_k1, lam_q2, lam_k2, subln_w, sgu_ln_g, sgu_ln_b,
           w_s, b_s, conv_w, w_out, final_norm_w):
    f = lambda a: np.asarray(a, dtype=np.float32)
    x = f(x)
    params = dict(w_in=f(w_in), w_out=f(w_out), norm_w=f(norm_w), final_norm_w=f(final_norm_w),
                  lam_q1=f(lam_q1), lam_k1=f(lam_k1), lam_q2=f(lam_q2), lam_k2=f(lam_k2),
                  subln_w=f(subln_w), sgu_ln_g=f(sgu_ln_g), sgu_ln_b=f(sgu_ln_b), w_s=f(w_s), b_s=f(b_s),
                  conv_w=f(conv_w))
    x_cores = [x[c // 2, (c % 2) * T:(c % 2 + 1) * T, :] for c in range(8)]
    if FUSED:
        outs = _run([0, 1], True, x_cores, params)
    else:
        mid = _run([0], False, x_cores, params)
        outs = _run([1], True, mid, params)
    out = np.empty((4, 4096, D), np.float32)
    for c in range(8):
        out[c // 2, (c % 2) * T:(c % 2 + 1) * T, :] = outs[c]
    return out
```
